# Optimizing a Trainium2 kernel written in Bass

```python
import jax, jax.numpy as jnp
from jax import lax
import numpy as np

D_MODEL = 2048
BATCH = 2
SEQ = 16384
DEPTH = 2

HEAD_DIM = 128
CHUNK = 128
A_GROUPS = D_MODEL // 256
D_A = A_GROUPS * HEAD_DIM
ATTN_PATTERNS = ((128, 1), (512, 4), (2048, 16))
HEADS_PER_GROUP = D_MODEL // 512
B_HEADS = HEADS_PER_GROUP * len(ATTN_PATTERNS)
D_B = B_HEADS * HEAD_DIM
D_B_OUT = HEADS_PER_GROUP * HEAD_DIM
D_FF = 4 * D_MODEL
Q_BLOCK = 128
ALIBI_MAX_EXP = 8.0
EPS = 1e-6
SPLITS = (D_A, 2 * D_A, 2 * D_A + D_B, 2 * D_A + 2 * D_B, 2 * D_A + 3 * D_B,
          2 * D_A + 3 * D_B + D_MODEL)
IN_COLS = 2 * D_A + 3 * D_B + 2 * D_MODEL

kernel_name = 'hybrid_gmlp_dilated_attn_gated_block'


def _rmsnorm(x, g):
    xf = x.astype(jnp.float32)
    y = xf * lax.rsqrt(jnp.mean(xf * xf, axis=-1, keepdims=True) + EPS)
    return (y * g.astype(jnp.float32)).astype(x.dtype)


def _layernorm(x, g, b):
    xf = x.astype(jnp.float32)
    mu = jnp.mean(xf, axis=-1, keepdims=True)
    var = jnp.mean(jnp.square(xf - mu), axis=-1, keepdims=True)
    y = (xf - mu) * lax.rsqrt(var + EPS)
    return (y * g.astype(jnp.float32) + b.astype(jnp.float32)).astype(x.dtype)


def _modulate(h, shift, scale):
    return h * (1 + scale[:, None, :]) + shift[:, None, :]


def _spatial_gating(u, v, w_s, b_s):
    bsz, seq, _ = v.shape
    n_chunks = seq // CHUNK
    causal = jnp.tril(jnp.ones((CHUNK, CHUNK), dtype=w_s.dtype))
    vc = v.reshape(bsz, n_chunks, CHUNK, A_GROUPS, HEAD_DIM)
    fv = jnp.einsum('gts,bnsgc->bntgc', w_s * causal, vc) + b_s.T[None, None, :, :, None]
    return u * fv.reshape(bsz, seq, D_A)


def _dilated_window_attention(q, k, v, window, dilation, slopes):
    bsz, seq, nh, hd = q.shape
    span = window // dilation
    m = seq // dilation
    nb = -(-m // Q_BLOCK)
    mp = nb * Q_BLOCK

    def to_blocks(t):
        t = t.reshape(bsz, m, dilation, nh, hd).transpose(0, 2, 3, 1, 4)
        t = jnp.pad(t, ((0, 0), (0, 0), (0, 0), (0, mp - m), (0, 0)))
        return t.reshape(bsz, dilation, nh, nb, Q_BLOCK, hd)

    def with_prev(t):
        prev = jnp.concatenate([jnp.zeros_like(t[:, :, :, :1]), t[:, :, :, :-1]], axis=3)
        return jnp.concatenate([prev, t], axis=4)

    qb = to_blocks(q * HEAD_DIM ** -0.5)
    kb = with_prev(to_blocks(k))
    vb = with_prev(to_blocks(v))
    scores = jnp.einsum('brhnqc,brhnkc->brhnqk', qb, kb, preferred_element_type=jnp.float32)
    qi = jnp.arange(Q_BLOCK)[:, None] + Q_BLOCK
    ki = jnp.arange(2 * Q_BLOCK)[None, :]
    dist = qi - ki
    blk = jnp.arange(nb)[:, None, None]
    valid = (dist >= 0) & (dist <= span) & (blk * Q_BLOCK + ki - Q_BLOCK >= 0)
    alibi = -slopes[:, None, None] * (dilation * dist).astype(jnp.float32)
    scores = jnp.where(valid[None, None, None], scores + alibi[None, None, :, None], -jnp.inf)
    lse = jax.nn.logsumexp(scores, axis=-1)
    probs = jnp.exp(scores - lse[..., None])
    out = jnp.einsum('brhnqk,brhnkc->brhnqc', probs, vb.astype(jnp.float32))
    out = out.reshape(bsz, dilation, nh, mp, hd)[:, :, :, :m]
    out = out.transpose(0, 3, 1, 2, 4).reshape(bsz, seq, nh, hd)
    lse = lse.reshape(bsz, dilation, nh, mp)[:, :, :, :m]
    lse = lse.transpose(0, 3, 1, 2).reshape(bsz, seq, nh)
    return out, lse


def _hybrid_mixer(h, w_in, b_in, g_v, b_v, w_s, b_s, w_oa, w_ob, w_out):
    bsz, seq, _ = h.shape
    proj = h @ w_in + b_in
    u_a, v_a, q, k, v, gate_a, gate_b = jnp.split(proj, SPLITS, axis=-1)
    y_a = _spatial_gating(jax.nn.gelu(u_a), _layernorm(jax.nn.gelu(v_a), g_v, b_v), w_s, b_s)
    q = q.reshape(bsz, seq, B_HEADS, HEAD_DIM)
    k = k.reshape(bsz, seq, B_HEADS, HEAD_DIM)
    v = v.reshape(bsz, seq, B_HEADS, HEAD_DIM)
    slopes = 2.0 ** (-ALIBI_MAX_EXP * jnp.arange(1, B_HEADS + 1, dtype=jnp.float32) / B_HEADS)
    outs, lses = [], []
    for g, (window, dilation) in enumerate(ATTN_PATTERNS):
        hs = slice(g * HEADS_PER_GROUP, (g + 1) * HEADS_PER_GROUP)
        o, l = _dilated_window_attention(q[:, :, hs], k[:, :, hs], v[:, :, hs],
                                         window, dilation, slopes[hs])
        outs.append(o)
        lses.append(l)
    weights = jax.nn.softmax(jnp.stack(lses), axis=0)
    y_b = jnp.sum(weights[..., None] * jnp.stack(outs), axis=0)
    y_b = y_b.reshape(bsz, seq, D_B_OUT).astype(h.dtype)
    merged = jax.nn.sigmoid(gate_a) * (y_a @ w_oa) + jax.nn.sigmoid(gate_b) * (y_b @ w_ob)
    return merged @ w_out


def _sq_relu_mlp(h, w1, b1, w2, b2):
    return jnp.square(jax.nn.relu(h @ w1 + b1)) @ w2 + b2


def setup_inputs(seed: int = 0) -> dict:
    key = jax.random.key(seed)
    ks = jax.random.split(key, 24)
    L = DEPTH
    f32 = jnp.float32

    def nrm(k, shape, fan_in):
        return jax.random.normal(k, shape, f32) * fan_in ** -0.5

    def small(k, shape):
        return 0.02 * jax.random.normal(k, shape, f32)

    return {
        'x': jax.random.normal(ks[0], (BATCH, SEQ, D_MODEL), f32),
        'c': jax.random.normal(ks[1], (BATCH, D_MODEL), f32),
        'w_ada': 0.5 * nrm(ks[2], (L, D_MODEL, 6 * D_MODEL), D_MODEL),
        'b_ada': small(ks[3], (L, 6 * D_MODEL)),
        'g_mix': 1.0 + small(ks[4], (L, D_MODEL)),
        'w_in': nrm(ks[5], (L, D_MODEL, IN_COLS), D_MODEL),
        'b_in': small(ks[6], (L, IN_COLS)),
        'g_v': 1.0 + small(ks[7], (L, D_A)),
        'b_v': small(ks[8], (L, D_A)),
        'w_s': nrm(ks[9], (L, A_GROUPS, CHUNK, CHUNK), CHUNK),
        'b_s': 1.0 + small(ks[10], (L, A_GROUPS, CHUNK)),
        'w_oa': nrm(ks[11], (L, D_A, D_MODEL), D_A),
        'w_ob': nrm(ks[12], (L, D_B_OUT, D_MODEL), D_B_OUT),
        'w_out': nrm(ks[13], (L, D_MODEL, D_MODEL), D_MODEL),
        'g_mlp': 1.0 + small(ks[14], (L, D_MODEL)),
        'w1': nrm(ks[15], (L, D_MODEL, D_FF), D_MODEL),
        'b1': small(ks[16], (L, D_FF)),
        'w2': nrm(ks[17], (L, D_FF, D_MODEL), D_FF),
        'b2': small(ks[18], (L, D_MODEL)),
        'g_final': 1.0 + small(ks[19], (D_MODEL,)),
    }


def reference(x, c, w_ada, b_ada, g_mix, w_in, b_in, g_v, b_v, w_s, b_s, w_oa, w_ob,
              w_out, g_mlp, w1, b1, w2, b2, g_final):
    c_act = jax.nn.silu(c)
    for l in range(DEPTH):
        mod = c_act @ w_ada[l] + b_ada[l]
        sh1, sc1, gt1, sh2, sc2, gt2 = jnp.split(mod, 6, axis=-1)
        h = _modulate(_rmsnorm(x, g_mix[l]), sh1, sc1)
        x = x + gt1[:, None, :] * _hybrid_mixer(h, w_in[l], b_in[l], g_v[l], b_v[l], w_s[l],
                                                b_s[l], w_oa[l], w_ob[l], w_out[l])
        h = _modulate(_rmsnorm(x, g_mlp[l]), sh2, sc2)
        x = x + gt2[:, None, :] * _sq_relu_mlp(h, w1[l], b1[l], w2[l], b2[l])
    return _rmsnorm(x, g_final)
```

```python
import math
from contextlib import ExitStack

import numpy as np
import concourse.bass as bass
import concourse.mybir as mybir
from concourse.bass_utils import run_bass_kernel_spmd

F32 = mybir.dt.float32
BF16 = mybir.dt.bfloat16
AF = mybir.ActivationFunctionType
ALU = mybir.AluOpType

D = 2048
DA = 1024
DBH = 1536
DFF = 8192
INC = 10752
OFF_U, OFF_VA, OFF_Q, OFF_K, OFF_V, OFF_GA, OFF_GB = 0, 1024, 2048, 3584, 5120, 6656, 8704
NDELTA = (1, 4, 16)
SLOPES = [2.0 ** (-8.0 * i / 12.0) for i in range(1, 13)]
EPS = 1e-6
QSCALE = 128.0 ** -0.5
N_CORES = 8
OWN = 4096
HALO = 2048


class KB:
    def __init__(self, nc):
        self.nc = nc
        self.E = dict(pe=nc.tensor, act=nc.scalar, dve=nc.vector, pool=nc.gpsimd, sp=nc.sync)
        self.gs = ExitStack()
        self.csem = {e: self.gs.enter_context(nc.semaphore(f"c_{e}")) for e in ("pe", "act", "dve", "pool")}
        self.ccnt = {e: 0 for e in self.csem}
        self.dsem = {}
        self.dcnt = {}
        self.dkey = {}
        self.dfree = []
        self.dfree_sw = []
        self.dsw = set()
        self.lastw = {}
        self.readers = {}
        self.waited = {e: {} for e in self.E}
        self.uid = 0
        self.ps = [self.gs.enter_context(nc.psum_tensor(f"psb{i}", [128, 512], F32)) for i in range(8)]
        self.psi = -1
        self.ph = None

    def psum(self):
        self.psi = (self.psi + 1) % 8
        return self.ps[self.psi], f"ps{self.psi}"

    def psum_at(self, i):
        return self.ps[i], f"ps{i}"

    def phase_begin(self, name):
        self.ph = ExitStack()
        self.pname = name

    def alloc(self, name, shape, dtype, persistent=False):
        self.uid += 1
        st = self.gs if persistent else self.ph
        return st.enter_context(self.nc.sbuf_tensor(f"{name}_{self.uid}", list(shape), dtype))

    def _dsem(self, key, eng):
        sw = eng == "pool"
        free = self.dfree_sw if sw else self.dfree
        if key not in self.dkey:
            if free:
                idx = free.pop()
            else:
                idx = len(self.dsem)
                self.dsem[idx] = self.gs.enter_context(self.nc.semaphore(f"d_{idx}"))
                self.dcnt[idx] = 0
                if sw:
                    self.dsw.add(idx)
            self.dkey[key] = idx
        assert (self.dkey[key] in self.dsw) == sw, key
        return self.dkey[key]

    def _wait(self, eng, ev):
        kind, key, val = ev
        w = self.waited[eng]
        if w.get((kind, key), 0) >= val:
            return
        if kind == "c" and key == "pe" and eng == "pe":
            return
        sem = self.csem[key] if kind == "c" else self.dsem[key]
        self.E[eng].wait_ge(sem, val)
        w[(kind, key)] = val

    def op(self, eng, fn, r=(), w=(), dma=None):
        evs = []
        for k in r:
            if k in self.lastw:
                evs.append(self.lastw[k])
        for k in w:
            if k in self.lastw:
                evs.append(self.lastw[k])
            evs.extend(self.readers.get(k, ()))
        for ev in evs:
            self._wait(eng, ev)
        ins = fn(self.E[eng])
        if dma is not None:
            idx = self._dsem(dma, eng)
            ins.then_inc(self.dsem[idx], 16)
            self.dcnt[idx] += 16
            ev = ("d", idx, self.dcnt[idx])
        else:
            ins.then_inc(self.csem[eng], 1)
            self.ccnt[eng] += 1
            ev = ("c", eng, self.ccnt[eng])
        for k in w:
            self.lastw[k] = ev
            self.readers[k] = []
        for k in r:
            if k not in w:
                self.readers.setdefault(k, []).append(ev)
        return ev

    def barrier(self):
        for eng in self.E:
            for e, c in self.ccnt.items():
                if c:
                    self._wait(eng, ("c", e, c))
            for k, c in self.dcnt.items():
                if c:
                    self._wait(eng, ("d", k, c))
        self.lastw = {}
        self.readers = {}
        for idx in self.dkey.values():
            (self.dfree_sw if idx in self.dsw else self.dfree).append(idx)
        self.dkey = {}

    def phase_end(self):
        self.barrier()
        self.ph.close()
        self.ph = None

    def finish(self):
        self.gs.close()


class Ring:
    def __init__(self, kb, name, n, shape, dtype):
        self.t = [kb.alloc(f"{name}{i}", shape, dtype) for i in range(n)]
        kb.uid += 1
        self.keys = [f"{name}{i}_{kb.uid}" for i in range(n)]
        self.i = -1
        self.n = n

    def next(self):
        self.i = (self.i + 1) % self.n
        return self.t[self.i], self.keys[self.i]


def mm_group(ps_ap, pairs):
    def fn(pe):
        ins = None
        n = len(pairs)
        for i, (a, b) in enumerate(pairs):
            ins = pe.matmul(ps_ap, lhsT=a, rhs=b, start=(i == 0), stop=(i == n - 1))
        return ins
    return fn


def load_bc(kb, dst, vec_ap, key):
    kb.op("sp", lambda e: e.dma_start(out=dst, in_=vec_ap.partition_broadcast(128)), w=[key], dma="ld_" + key)


def phase_mod(kb, L, S):
    kb.phase_begin("mod")
    cs = kb.alloc("cs", [128, 16], F32)
    crep = kb.alloc("crep", [128, 16, 128], F32)
    kb.op("sp", lambda e: e.dma_start(out=cs[:], in_=L["cT"]), w=["cs"], dma="ld_cs")
    kb.op("act", lambda e: e.activation(out=cs[:], in_=cs[:], func=AF.Silu), r=["cs"], w=["cs"])
    for k in range(16):
        kb.op("dve", lambda e, k=k: e.tensor_copy(out=crep[:, k, :], in_=cs[:, k:k + 1].to_broadcast([128, 128])),
              r=["cs"], w=[f"crep{k}"])
    wr = Ring(kb, "wada", 3, [128, 16, 512], F32)
    br = Ring(kb, "bada", 2, [128, 512], F32)
    orr = Ring(kb, "modo", 2, [128, 512], F32)
    wsrc = L["w_ada"].rearrange("(k p) c -> p k c", p=128)
    for cg in range(24):
        wt, wk = wr.next()
        bt, bk = br.next()
        ot, ok = orr.next()
        cs_ = slice(cg * 512, (cg + 1) * 512)
        kb.op("sp", lambda e: e.dma_start(out=wt[:], in_=wsrc[:, :, cs_]), w=[wk], dma="ld_" + wk)
        load_bc(kb, bt[:], L["b_ada"][cs_], bk)
        ps, pk = kb.psum()
        kb.op("pe", mm_group(ps[:], [(crep[:, k, :], wt[:, k, :]) for k in range(16)]),
              r=[wk] + [f"crep{k}" for k in range(16)], w=[pk])
        kb.op("dve", lambda e: e.tensor_tensor(out=ot[:], in0=ps[:], in1=bt[:], op=ALU.add), r=[pk, bk], w=[ok])
        kb.op("act", lambda e: e.dma_start(out=S["modbc"][:, cs_], in_=ot[:]), r=[ok], dma="st_" + ok)
    kb.phase_end()


def load_gm_sh(kb, L, S, gname, sc_idx, sh_idx):
    gm = kb.alloc("gm", [128, D], F32)
    gb = kb.alloc("gbc", [128, D], F32)
    load_bc(kb, gb[:], L[gname], "gbc")
    if sc_idx is not None:
        kb.op("sp", lambda e: e.dma_start(out=gm[:], in_=S["modbc"][:, sc_idx * D:(sc_idx + 1) * D]), w=["gm"], dma="ld_gm")
        kb.op("dve", lambda e: e.scalar_tensor_tensor(out=gm[:], in0=gm[:], scalar=1.0, in1=gb[:], op0=ALU.add, op1=ALU.mult),
              r=["gm", "gbc"], w=["gm"])
    else:
        kb.op("dve", lambda e: e.tensor_copy(out=gm[:], in_=gb[:]), r=["gbc"], w=["gm"])
    sh = None
    if sh_idx is not None:
        sh = kb.alloc("sh", [128, D], F32)
        kb.op("sp", lambda e: e.dma_start(out=sh[:], in_=S["modbc"][:, sh_idx * D:(sh_idx + 1) * D]), w=["sh"], dma="ld_sh")
    return gm, sh


def phase_norm(kb, L, S, name, src, t0, t1, gname, sc_idx, sh_idx, dstT=None, dst_final=None, dst_off=0):
    kb.phase_begin(name)
    gm, sh = load_gm_sh(kb, L, S, gname, sc_idx, sh_idx)
    ident = S["identb"]
    xr = Ring(kb, "nx", 8, [128, D], F32)
    jt = kb.alloc("nj", [128, D], BF16)
    sr = Ring(kb, "ns", 2, [128, 8], F32)
    tr = Ring(kb, "nt", 3, [128, D], F32)
    hr = Ring(kb, "nh", 3, [128, D], BF16)
    hTr = Ring(kb, "nhT", 2, [128, 16, 512], BF16)
    dT = dstT.rearrange("(k p) t -> p k t", p=128) if dstT is not None else None
    for ct in range(t0 // 512, t1 // 512):
        if dstT is not None:
            hT, hTk = hTr.next()
        st, sk = sr.next()
        xs_ = []
        for r4 in range(4):
            tok = ct * 512 + r4 * 128
            xt, xk = xr.next()
            xs_.append((xt, xk))
            kb.op("sp", lambda e: e.dma_start(out=xt[:], in_=src[tok:tok + 128, :]), w=[xk], dma="ld_" + xk)
            kb.op("act", lambda e: e.activation(out=jt[:], in_=xt[:], func=AF.Square, accum_out=st[:, r4:r4 + 1]),
                  r=[xk], w=["nj", sk])
        kb.op("act", lambda e: e.activation(out=st[:, 4:8], in_=st[:, 0:4], func=AF.Sqrt, bias=EPS, scale=1.0 / D), r=[sk], w=[sk])
        kb.op("dve", lambda e: e.reciprocal(out=st[:, 4:8], in_=st[:, 4:8]), r=[sk], w=[sk])
        for r4 in range(4):
            tok = ct * 512 + r4 * 128
            xt, xk = xs_[r4]
            tt, tk = tr.next()
            kb.op("dve", lambda e: e.scalar_tensor_tensor(out=tt[:], in0=xt[:], scalar=st[:, 4 + r4:5 + r4], in1=gm[:],
                                                           op0=ALU.mult, op1=ALU.mult), r=[xk, sk, "gm"], w=[tk])
            if dstT is None:
                kb.op("pool", lambda e: e.dma_start(out=dst_final[tok - dst_off:tok - dst_off + 128, :], in_=tt[:]),
                      r=[tk], dma="st_" + tk)
                continue
            hb, hk = hr.next()
            kb.op("pool", lambda e: e.tensor_tensor(out=hb[:, 0:1024], in0=tt[:, 0:1024], in1=sh[:, 0:1024], op=ALU.add), r=[tk, "sh"], w=[hk + "a"])
            kb.op("dve", lambda e: e.tensor_tensor(out=hb[:, 1024:2048], in0=tt[:, 1024:2048], in1=sh[:, 1024:2048], op=ALU.add), r=[tk, "sh"], w=[hk + "b"])
            for half in range(2):
                ps, pk = kb.psum()
                pv = ps[:].bitcast(BF16).rearrange("p (k t) -> p k t", t=128)

                def tfn(pe, half=half, pv=pv, hb=hb):
                    ins = None
                    for k in range(8):
                        kk = half * 8 + k
                        ins = pe.transpose(pv[:, k, :], hb[:, kk * 128:(kk + 1) * 128], ident[:])
                    return ins
                kb.op("pe", tfn, r=[hk + ("a" if half == 0 else "b")], w=[pk])
                dst = hT[:, half * 8:(half + 1) * 8, r4 * 128:(r4 + 1) * 128]
                if half == 0:
                    kb.op("act", lambda e, dst=dst, pv=pv: e.copy(out=dst, in_=pv), r=[pk], w=[hTk])
                else:
                    kb.op("dve", lambda e, dst=dst, pv=pv: e.tensor_copy(out=dst, in_=pv), r=[pk], w=[hTk])
        if dstT is not None:
            kb.op("act", lambda e: e.dma_start(out=dT[:, :, ct * 512:(ct + 1) * 512], in_=hT[:]), r=[hTk], dma="st_" + hTk)
    kb.phase_end()


def gemm_fm(kb, name, inT, K, W, biasT, tok1, jobs, post_sq=False):
    kb.phase_begin(name)
    KC = K // 128
    CG = 1024
    wr = Ring(kb, "fw", 2, [128, KC, CG], BF16)
    ar = Ring(kb, "fa", 2, [128, KC, 512], BF16)
    orr = Ring(kb, "fo", 2, [128, 8, 512], BF16)
    br = Ring(kb, "fb", 2, [128, 8], F32)
    rr = Ring(kb, "fr", 2, [128, 512], F32) if post_sq else None
    wsrc = W.rearrange("(k p) c -> p k c", p=128)
    isrc = inT.rearrange("(k p) t -> p k t", p=128)
    stq = "pool" if post_sq else "act"
    groups = []
    for (c0, ncols, tok0, func, scale, outT) in jobs:
        odst = outT.rearrange("(c p) t -> p c t", p=128)
        for cg in range((ncols + CG - 1) // CG):
            groups.append((c0 + cg * CG, min(CG, ncols - cg * CG), tok0, func, scale, odst, (cg * CG) // 128))

    def load_group(gi):
        cc0, ncg, tok0, func, scale, odst, r0 = groups[gi]
        nch = ncg // 128
        wt, wk = wr.next()
        bt, bk = br.next()
        kb.op("pool", lambda e: e.dma_start(out=wt[:, :, 0:ncg], in_=wsrc[:, :, cc0:cc0 + ncg]), w=[wk], dma="ld_" + wk)
        kb.op("sp", lambda e: e.dma_start(out=bt[:, 0:nch], in_=biasT[:, cc0 // 128:cc0 // 128 + nch]), w=[bk], dma="ld_" + bk)
        if scale != 1.0:
            kb.op("dve", lambda e: e.tensor_scalar(out=bt[:, 0:nch], in0=bt[:, 0:nch], scalar1=float(scale), scalar2=None,
                                                    op0=ALU.mult), r=[bk], w=[bk])
        return wt, wk, bt, bk, nch

    nxt = load_group(0)
    for gi in range(len(groups)):
        wt, wk, bt, bk, nch = nxt
        cc0, ncg, tok0, func, scale, odst, r0 = groups[gi]
        for ct in range(tok0 // 512, tok1 // 512):
            at, ak = ar.next()
            ot, ok = orr.next()
            kb.op("sp", lambda e: e.dma_start(out=at[:], in_=isrc[:, :, ct * 512:(ct + 1) * 512]), w=[ak], dma="ld_" + ak)
            for cc in range(nch):
                ps, pk = kb.psum()
                kb.op("pe", mm_group(ps[:], [(wt[:, k, cc * 128:(cc + 1) * 128], at[:, k, :]) for k in range(KC)]),
                      r=[wk, ak], w=[pk])
                if post_sq:
                    rt, rk = rr.next()
                    kb.op("act", lambda e: e.activation(out=rt[:], in_=ps[:], func=func, bias=bt[:, cc:cc + 1], scale=float(scale)),
                          r=[pk, bk], w=[rk])
                    kb.op("pool", lambda e: e.tensor_tensor(out=ot[:, cc, :], in0=rt[:], in1=rt[:], op=ALU.mult), r=[rk], w=[ok])
                else:
                    kb.op("act", lambda e: e.activation(out=ot[:, cc, :], in_=ps[:], func=func, bias=bt[:, cc:cc + 1], scale=float(scale)),
                          r=[pk, bk], w=[ok])
            kb.op(stq, lambda e: e.dma_start(out=odst[:, r0:r0 + nch, ct * 512:(ct + 1) * 512], in_=ot[:, 0:nch, :]),
                  r=[ok], dma="st_" + ok)
            if ct == tok0 // 512 and gi + 1 < len(groups):
                nxt = load_group(gi + 1)
    kb.phase_end()


def gemm_tm(kb, inT, K, W, c0, CGc, ngroups, tok0, tok1, TT, pre, epi):
    KC = K // 128
    NQ = 4
    KQ = KC // NQ
    wt = kb.alloc("tw", [128, KC, CGc], BF16)
    ar = Ring(kb, "ta", 2, [128, KC, TT], BF16)
    wsrc = W.rearrange("(k p) c -> p k c", p=128)
    isrc = inT.rearrange("(k p) t -> p k t", p=128)
    for cg in range(ngroups):
        cc0 = c0 + cg * CGc
        for qi in range(NQ):
            ks = slice(qi * KQ, (qi + 1) * KQ)
            kb.op("pool", lambda e, ks=ks: e.dma_start(out=wt[:, ks, :], in_=wsrc[:, ks, cc0:cc0 + CGc]), w=[f"tw{qi}"], dma=f"ld_tw{qi}")
        for tt in range(tok0 // TT, tok1 // TT):
            at, ak = ar.next()
            kb.op("sp", lambda e: e.dma_start(out=at[:], in_=isrc[:, :, tt * TT:(tt + 1) * TT]), w=[ak], dma="ld_" + ak)
            for r in range(TT // 128):
                tok = tt * TT + r * 128
                if pre is not None:
                    pre(tok, cg)
                for cb in range(CGc // 512):
                    ps, pk = kb.psum()
                    for qi in range(NQ):
                        def fn(pe, qi=qi, ps=ps, at=at, r=r, cb=cb):
                            ins = None
                            for k in range(qi * KQ, (qi + 1) * KQ):
                                ins = pe.matmul(ps[:], lhsT=at[:, k, r * 128:(r + 1) * 128], rhs=wt[:, k, cb * 512:(cb + 1) * 512],
                                                start=(k == 0), stop=(k == KC - 1))
                            return ins
                        kb.op("pe", fn, r=[f"tw{qi}", ak], w=[pk])
                    epi(tok, cg * (CGc // 512) + cb, ps, pk)


def phase_vproj(kb, L, S, FS, NT):
    kb.phase_begin("vproj")
    bvb = kb.alloc("bvb", [128, DBH], F32)
    load_bc(kb, bvb[:], L["b_in"][OFF_V:OFF_V + DBH], "bvb")
    vr = Ring(kb, "vt", 2, [128, DBH], BF16)
    cur = {}

    def pre(tok, cg):
        cur["t"], cur["k"] = vr.next()

    def epi_a(tok, cb, ps, pk):
        vt, vk = cur["t"], cur["k"]
        kb.op("dve", lambda e: e.tensor_tensor(out=vt[:, 1024:1536], in0=ps[:], in1=bvb[:, 1024:1536], op=ALU.add),
              r=[pk, "bvb"], w=[vk])
        kb.op("act", lambda e: e.dma_start(out=S["V"][tok:tok + 128, 1024:1536], in_=vt[:, 1024:1536]), r=[vk], dma="st_" + vk)

    def epi_b(tok, cb, ps, pk):
        vt, vk = cur["t"], cur["k"]
        kb.op("dve", lambda e: e.tensor_tensor(out=vt[:, cb * 512:(cb + 1) * 512], in0=ps[:], in1=bvb[:, cb * 512:(cb + 1) * 512], op=ALU.add),
              r=[pk, "bvb"], w=[vk])
        if cb == 2:
            kb.op("act", lambda e: e.dma_start(out=S["V"][tok:tok + 128, :], in_=vt[:]), r=[vk], dma="st_" + vk)

    gemm_tm(kb, S["hT"], D, L["w_in"], OFF_V + 1024, 512, 1, 0, FS - 512, 512, pre, epi_a)
    gemm_tm(kb, S["hT"], D, L["w_in"], OFF_V, DBH, 1, FS - 512, NT, 512, pre, epi_b)
    kb.phase_end()


def phase_gating(kb, L, S, FS, NT):
    kb.phase_begin("gating")
    ident = S["identb"]
    wsn = kb.alloc("wsn", [128, 8, 128], F32)
    wsb = kb.alloc("wsb", [128, 8, 128], BF16)
    wmT = kb.alloc("wmT", [128, 8, 128], BF16)
    kb.op("sp", lambda e: e.dma_start(out=wsn[:], in_=L["w_s"].rearrange("g t s -> t g s")), w=["wsn"], dma="ld_wsn")
    kb.op("dve", lambda e: e.tensor_tensor(out=wsb[:], in0=wsn[:], in1=S["tril"][:].unsqueeze(1).to_broadcast([128, 8, 128]), op=ALU.mult),
          r=["wsn"], w=["wsb"])
    ps, pk = kb.psum()
    pv = ps[:].bitcast(BF16).rearrange("p (k t) -> p k t", t=128)

    def tfn(pe):
        ins = None
        for g in range(8):
            ins = pe.transpose(pv[:, g, :], wsb[:, g, :], ident[:])
        return ins
    kb.op("pe", tfn, r=["wsb"], w=[pk])
    kb.op("dve", lambda e: e.tensor_copy(out=wmT[:], in_=pv), r=[pk], w=["wmT"])
    bsb = kb.alloc("bsb", [128, 8, 128], F32)
    load_bc(kb, bsb[:].rearrange("p g t -> p (g t)"), L["b_s"].rearrange("g t -> (g t)"), "bsb")
    bvab = kb.alloc("bvab", [128, DA], F32)
    load_bc(kb, bvab[:], L["b_in"][OFF_VA:OFF_VA + DA], "bvab")
    gvb = kb.alloc("gvb", [128, DA], F32)
    load_bc(kb, gvb[:], L["g_v"], "gvb")
    bvb2 = kb.alloc("bvb2", [128, DA], F32)
    load_bc(kb, bvb2[:], L["b_v"], "bvb2")
    ur = Ring(kb, "gu", 2, [128, 8, 512], BF16)
    yr = Ring(kb, "gy", 2, [128, 8, 512], BF16)
    t1r = Ring(kb, "gt1", 2, [128, DA], F32)
    t2r = Ring(kb, "gt2", 2, [128, DA], F32)
    vnr = Ring(kb, "gvn", 2, [128, DA], BF16)
    str_ = Ring(kb, "gst", 2, [128, 16], F32)
    jt = kb.alloc("gjunk", [128, DA], BF16)
    usrc = S["uT"].rearrange("(g p) t -> p g t", p=128)
    ydst = S["yaT"].rearrange("(g p) t -> p g t", p=128)
    cur = {}

    def pre(tok, cg):
        r4 = (tok // 128) % 4
        if r4 == 0:
            ut, uk = ur.next()
            yt, yk = yr.next()
            cur.update(ut=ut, uk=uk, yt=yt, yk=yk)
            ct = tok // 512
            kb.op("sp", lambda e: e.dma_start(out=ut[:], in_=usrc[:, :, ct * 512:(ct + 1) * 512]), w=[uk], dma="ld_" + uk)
        t1, t1k = t1r.next()
        cur.update(t1=t1, t1k=t1k)

    def epi(tok, cb, ps, pk):
        t1, t1k = cur["t1"], cur["t1k"]
        kb.op("dve", lambda e: e.tensor_tensor(out=t1[:, cb * 512:(cb + 1) * 512], in0=ps[:], in1=bvab[:, cb * 512:(cb + 1) * 512], op=ALU.add),
              r=[pk, "bvab"], w=[t1k])
        if cb != 1:
            return
        r4 = (tok // 128) % 4
        ut, uk, yt, yk = cur["ut"], cur["uk"], cur["yt"], cur["yk"]
        t2, t2k = t2r.next()
        vn, vnk = vnr.next()
        st, sk = str_.next()
        kb.op("act", lambda e: e.activation(out=t1[:], in_=t1[:], func=AF.Gelu_apprx_tanh, accum_out=st[:, 0:1]), r=[t1k], w=[t1k, sk])

        kb.op("act", lambda e: e.activation(out=jt[:], in_=t1[:], func=AF.Square, accum_out=st[:, 1:2]), r=[t1k], w=[sk, "gjunk"])
        kb.op("dve", lambda e: e.tensor_scalar(out=st[:, 12:13], in0=st[:, 0:1], scalar1=1.0 / DA, scalar2=None, op0=ALU.mult), r=[sk], w=[sk])
        kb.op("dve", lambda e: e.tensor_tensor(out=st[:, 3:4], in0=st[:, 12:13], in1=st[:, 12:13], op=ALU.mult), r=[sk], w=[sk])
        kb.op("dve", lambda e: e.scalar_tensor_tensor(out=st[:, 13:14], in0=st[:, 1:2], scalar=1.0 / DA, in1=st[:, 3:4],
                                                       op0=ALU.mult, op1=ALU.subtract), r=[sk], w=[sk])
        kb.op("act", lambda e: e.activation(out=st[:, 14:15], in_=st[:, 13:14], func=AF.Sqrt, bias=EPS, scale=1.0), r=[sk], w=[sk])
        kb.op("dve", lambda e: e.reciprocal(out=st[:, 14:15], in_=st[:, 14:15]), r=[sk], w=[sk])
        kb.op("dve", lambda e: e.scalar_tensor_tensor(out=t2[:], in0=t1[:], scalar=st[:, 12:13], in1=gvb[:], op0=ALU.subtract, op1=ALU.mult),
              r=[t1k, sk, "gvb"], w=[t2k])
        kb.op("dve", lambda e: e.scalar_tensor_tensor(out=vn[:], in0=t2[:], scalar=st[:, 14:15], in1=bvb2[:], op0=ALU.mult, op1=ALU.add),
              r=[t2k, sk, "bvb2"], w=[vnk])
        for half in range(2):
            psg, pgk = kb.psum()
            pg = psg[:].rearrange("p (g t) -> p g t", t=128)

            def gfn(pe, half=half, pg=pg, vn=vn):
                ins = None
                for gi in range(4):
                    g = half * 4 + gi
                    ins = pe.matmul(pg[:, gi, :], lhsT=vn[:, g * 128:(g + 1) * 128], rhs=wmT[:, g, :], start=True, stop=True)
                return ins
            kb.op("pe", gfn, r=[vnk, "wmT"], w=[pgk])
            gs = slice(half * 4, half * 4 + 4)
            tsl = slice(r4 * 128, (r4 + 1) * 128)
            t2v = t2[:, half * 512:(half + 1) * 512].rearrange("p (g t) -> p g t", t=128)
            kb.op("dve", lambda e, pg=pg, gs=gs, t2v=t2v: e.tensor_tensor(out=t2v, in0=pg, in1=bsb[:, gs, :], op=ALU.add),
                  r=[pgk, "bsb", vnk], w=[t2k])
            kb.op("pool", lambda e, gs=gs, tsl=tsl, t2v=t2v: e.tensor_tensor(out=yt[:, gs, tsl], in0=t2v, in1=ut[:, gs, tsl], op=ALU.mult),
                  r=[t2k, uk], w=[yk])
        if r4 == 3:
            ct = tok // 512
            kb.op("pool", lambda e: e.dma_start(out=ydst[:, :, ct * 512:(ct + 1) * 512], in_=yt[:]), r=[yk], dma="st_" + yk)

    gemm_tm(kb, S["hT"], D, L["w_in"], OFF_VA, DA, 1, FS, NT, 512, pre, epi)
    kb.phase_end()


NMW_OFF = (0, 256, 256 + 640)
NMW_COLS = 256 + 640 + 2176


def phase_attn(kb, L, S, FS, NT, own_tile):
    kb.phase_begin("attn")
    nmw = S["nmw"]
    cb = S["cb"]
    ones = S["onesb"]
    zeros = S["zerosb"]
    qr = Ring(kb, "aq", 2, [128, 3, 512], BF16)
    kr = Ring(kb, "ak", 2, [128, 3, 20 * 128], BF16)
    vr = Ring(kb, "av", 2, [128, 3, 20, 128], BF16)
    tr = Ring(kb, "at", 4, [128, 512], F32)
    pr = Ring(kb, "ap", 6, [128, 512], BF16)
    rr = Ring(kb, "ar", 2, [128, 512], F32)
    yr = Ring(kb, "ay", 2, [128, 4, 512], BF16)
    qsrc = S["qT"].rearrange("(h p) t -> p h t", p=128)
    ksrc = S["kT"].rearrange("(h p) t -> p h t", p=128)
    vsrc = S["V"].rearrange("(n p) (h c) -> p n h c", p=128, c=128)
    ydst = S["ybT"].rearrange("(j p) t -> p j t", p=128)
    acc_i = 0
    s_i = 0
    for st_ in range(FS // 512, NT // 512):
        yt, yk = yr.next()
        q0 = st_ * 4
        for j in range(4):
            qt, qk = qr.next()
            kt, kk_ = kr.next()
            vt, vk = vr.next()
            for g in range(3):
                h = g * 4 + j
                nd = NDELTA[g]
                k0 = q0 - nd
                nkt = nd + 4
                kb.op("sp", lambda e, g=g, h=h: e.dma_start(out=qt[:, g, :], in_=qsrc[:, h, st_ * 512:(st_ + 1) * 512]),
                      w=[qk], dma="ld_" + qk)
                kb.op("sp", lambda e, g=g, h=h, k0=k0, nkt=nkt: e.dma_start(out=kt[:, g, 0:nkt * 128], in_=ksrc[:, h, k0 * 128:(k0 + nkt) * 128]),
                      w=[kk_], dma="ld_" + kk_)
                kb.op("sp", lambda e, g=g, h=h, k0=k0, nkt=nkt: e.dma_start(out=vt[:, g, 0:nkt, :], in_=vsrc[:, k0:k0 + nkt, h, :]),
                      w=[vk], dma="ld_" + vk)
            acc_i = (acc_i + 1) % 2
            psn, pnk = kb.psum_at(2 * acc_i)
            psd, pdk = kb.psum_at(2 * acc_i + 1)
            kb.op("pe", lambda pe: pe.matmul(psn[:], lhsT=zeros[:], rhs=qt[:, 0, :], start=True, stop=False), r=[qk], w=[pnk])
            kb.op("pe", lambda pe: pe.matmul(psd[:], lhsT=zeros[:], rhs=qt[:, 0, :], start=True, stop=False), r=[qk], w=[pdk])
            steps = []
            for g in range(3):
                for kl in range(NDELTA[g] + 4):
                    steps.append((g, kl))
            nsteps = len(steps)
            pend = []
            done = 0

            def emit_pv(item, last):
                g, kl, qa, nq, pt, pkk = item
                cs = slice(qa * 128, (qa + nq) * 128)
                kb.op("pe", lambda pe: pe.matmul(psn[:, cs], lhsT=vt[:, g, kl, :], rhs=pt[:, 0:nq * 128], start=False, stop=last),
                      r=[pkk, vk], w=[pnk])
                kb.op("pe", lambda pe: pe.matmul(psd[:, cs], lhsT=ones[:], rhs=pt[:, 0:nq * 128], start=False, stop=last),
                      r=[pkk], w=[pdk])

            for (g, kl) in steps:
                h = g * 4 + j
                nd = NDELTA[g]
                qa = max(0, kl - nd)
                qb = min(3, kl)
                nq = qb - qa + 1
                w_ = nq * 128
                c0 = NMW_OFF[g] + (qa + nd - kl) * 128
                ki = q0 - nd + kl
                bcol = 1 if ki < own_tile else 0
                s_i = (s_i + 1) % 4
                pss, psk = kb.psum_at(4 + s_i)
                tt, tk = tr.next()
                pt, pkk = pr.next()
                kb.op("pe", lambda pe, g=g, kl=kl, qa=qa, w_=w_, pss=pss: pe.matmul(
                    pss[:, 0:w_], lhsT=kt[:, g, kl * 128:(kl + 1) * 128], rhs=qt[:, g, qa * 128:qa * 128 + w_], start=True, stop=True),
                    r=[kk_, qk], w=[psk])
                kb.op("dve", lambda e, c0=c0, w_=w_, h=h, pss=pss, tt=tt: e.scalar_tensor_tensor(
                    out=tt[:, 0:w_], in0=nmw[:, c0:c0 + w_], scalar=float(SLOPES[h]), in1=pss[:, 0:w_], op0=ALU.mult, op1=ALU.add),
                    r=[psk], w=[tk])
                kb.op("act", lambda e, bcol=bcol, w_=w_, tt=tt, pt=pt: e.activation(
                    out=pt[:, 0:w_], in_=tt[:, 0:w_], func=AF.Exp, bias=cb[:, bcol:bcol + 1], scale=1.0), r=[tk], w=[pkk])
                pend.append((g, kl, qa, nq, pt, pkk))
                if len(pend) > 3:
                    emit_pv(pend.pop(0), False)
                    done += 1
            while pend:
                done += 1
                emit_pv(pend.pop(0), done == nsteps)
            rt, rk = rr.next()
            kb.op("dve", lambda e: e.tensor_scalar(out=rt[:], in0=psd[:], scalar1=1e-30, scalar2=None, op0=ALU.add), r=[pdk], w=[rk])
            kb.op("dve", lambda e: e.reciprocal(out=rt[:], in_=rt[:]), r=[rk], w=[rk])
            kb.op("dve", lambda e: e.tensor_tensor(out=yt[:, j, :], in0=psn[:], in1=rt[:], op=ALU.mult), r=[pnk, rk], w=[yk])
        kb.op("pool", lambda e: e.dma_start(out=ydst[:, :, st_ * 512:(st_ + 1) * 512], in_=yt[:]), r=[yk], dma="st_" + yk)
    kb.phase_end()


def phase_merge(kb, L, S, FS, NT):
    kb.phase_begin("merge")
    woa = kb.alloc("woa", [128, 8, D], BF16)
    wob = kb.alloc("wob", [128, 4, D], BF16)
    kb.op("pool", lambda e: e.dma_start(out=woa[:], in_=L["w_oa"].rearrange("(k p) c -> p k c", p=128)), w=["woa"], dma="ld_woa")
    kb.op("pool", lambda e: e.dma_start(out=wob[:], in_=L["w_ob"].rearrange("(k p) c -> p k c", p=128)), w=["wob"], dma="ld_wob")
    yar = Ring(kb, "mya", 2, [128, 8, 512], BF16)
    ybr = Ring(kb, "myb", 2, [128, 4, 512], BF16)
    sar = Ring(kb, "msa", 2, [128, 4, 512], BF16)
    sbr = Ring(kb, "msb", 2, [128, 4, 512], BF16)
    t1r = Ring(kb, "mt1", 2, [128, 512], F32)
    t2r = Ring(kb, "mt2", 2, [128, 512], F32)
    mr = Ring(kb, "mm", 2, [128, 16, 512], BF16)
    yasrc = S["yaT"].rearrange("(k p) t -> p k t", p=128)
    ybsrc = S["ybT"].rearrange("(k p) t -> p k t", p=128)
    sgsrc = S["sgT"].rearrange("(k p) t -> p k t", p=128)
    mdst = S["mT"].rearrange("(k p) t -> p k t", p=128)
    for ct in range(FS // 512, NT // 512):
        ts = slice(ct * 512, (ct + 1) * 512)
        ya, yak = yar.next()
        yb, ybk = ybr.next()
        mt, mk = mr.next()
        kb.op("sp", lambda e: e.dma_start(out=ya[:], in_=yasrc[:, :, ts]), w=[yak], dma="ld_" + yak)
        kb.op("sp", lambda e: e.dma_start(out=yb[:], in_=ybsrc[:, :, ts]), w=[ybk], dma="ld_" + ybk)
        for fg in range(4):
            sa, sak = sar.next()
            sb, sbk = sbr.next()
            kb.op("sp", lambda e: e.dma_start(out=sa[:], in_=sgsrc[:, fg * 4:fg * 4 + 4, ts]), w=[sak], dma="ld_" + sak)
            kb.op("sp", lambda e: e.dma_start(out=sb[:], in_=sgsrc[:, 16 + fg * 4:16 + fg * 4 + 4, ts]), w=[sbk], dma="ld_" + sbk)
            for fc in range(4):
                f = fg * 4 + fc
                fs = slice(f * 128, (f + 1) * 128)
                psa, pak = kb.psum()
                psb, pbk = kb.psum()
                t1, t1k = t1r.next()
                t2, t2k = t2r.next()
                kb.op("pe", mm_group(psa[:], [(woa[:, k, fs], ya[:, k, :]) for k in range(8)]), r=["woa", yak], w=[pak])
                kb.op("pe", mm_group(psb[:], [(wob[:, k, fs], yb[:, k, :]) for k in range(4)]), r=["wob", ybk], w=[pbk])
                kb.op("dve", lambda e, fc=fc: e.tensor_tensor(out=t1[:], in0=psa[:], in1=sa[:, fc, :], op=ALU.mult), r=[pak, sak], w=[t1k])
                kb.op("dve", lambda e, fc=fc: e.tensor_tensor(out=t2[:], in0=psb[:], in1=sb[:, fc, :], op=ALU.mult), r=[pbk, sbk], w=[t2k])
                kb.op("pool", lambda e, f=f: e.tensor_tensor(out=mt[:, f, :], in0=t1[:], in1=t2[:], op=ALU.add), r=[t1k, t2k], w=[mk])
        kb.op("pool", lambda e: e.dma_start(out=mdst[:, :, ts], in_=mt[:]), r=[mk], dma="st_" + mk)
    kb.phase_end()


def phase_outproj(kb, L, S, FS, NT, xsrc, xmid):
    kb.phase_begin("outproj")
    gt = kb.alloc("gt1", [128, D], F32)
    kb.op("sp", lambda e: e.dma_start(out=gt[:], in_=S["modbc"][:, 2 * D:3 * D]), w=["gt1"], dma="ld_gt1")
    xr = Ring(kb, "ox", 2, [128, D], F32)
    tr = Ring(kb, "ot", 2, [128, 512], F32)
    cur = {}

    def pre(tok, cg):
        xt, xk = xr.next()
        cur.update(xt=xt, xk=xk)
        kb.op("sp", lambda e: e.dma_start(out=xt[:], in_=xsrc[tok:tok + 128, :]), w=[xk], dma="ld_" + xk)

    def epi(tok, cb, ps, pk):
        xt, xk = cur["xt"], cur["xk"]
        tt, tk = tr.next()
        cs = slice(cb * 512, (cb + 1) * 512)
        kb.op("dve", lambda e: e.tensor_tensor(out=tt[:], in0=ps[:], in1=gt[:, cs], op=ALU.mult), r=[pk, "gt1"], w=[tk])
        kb.op("pool", lambda e: e.tensor_tensor(out=xt[:, cs], in0=xt[:, cs], in1=tt[:], op=ALU.add), r=[tk, xk], w=[xk])
        if cb == 3:
            kb.op("pool", lambda e: e.dma_start(out=xmid[tok:tok + 128, :], in_=xt[:]), r=[xk], dma="st_" + xk)

    gemm_tm(kb, S["mT"], D, L["w_out"], 0, D, 1, FS, NT, 512, pre, epi)
    kb.phase_end()


def phase_mlp2(kb, L, S, FS, NT, xmid, xfin, xfin_off):
    kb.phase_begin("mlp2")
    gt = kb.alloc("gt2", [128, D], F32)
    b2 = kb.alloc("b2b", [128, D], F32)
    kb.op("sp", lambda e: e.dma_start(out=gt[:], in_=S["modbc"][:, 5 * D:6 * D]), w=["gt2"], dma="ld_gt2")
    load_bc(kb, b2[:], L["b2"], "b2b")
    xr = Ring(kb, "hx", 3, [128, 512], F32)
    tr = Ring(kb, "ht", 2, [128, 512], F32)
    cur = {}

    def pre(tok, cg):
        xt, xk = xr.next()
        cur.update(xt=xt, xk=xk)
        kb.op("sp", lambda e: e.dma_start(out=xt[:], in_=xmid[tok:tok + 128, cg * 512:(cg + 1) * 512]), w=[xk], dma="ld_" + xk)

    def epi(tok, cb, ps, pk):
        xt, xk = cur["xt"], cur["xk"]
        tt, tk = tr.next()
        cs = slice(cb * 512, (cb + 1) * 512)
        kb.op("dve", lambda e: e.tensor_tensor(out=tt[:], in0=ps[:], in1=b2[:, cs], op=ALU.add), r=[pk, "b2b"], w=[tk])
        kb.op("pool", lambda e: e.tensor_tensor(out=tt[:], in0=tt[:], in1=gt[:, cs], op=ALU.mult), r=[tk, "gt2"], w=[tk])
        kb.op("dve", lambda e: e.tensor_tensor(out=xt[:], in0=xt[:], in1=tt[:], op=ALU.add), r=[tk, xk], w=[xk])
        kb.op("act", lambda e: e.dma_start(out=xfin[tok - xfin_off:tok - xfin_off + 128, cs], in_=xt[:]), r=[xk], dma="st_" + xk)

    gemm_tm(kb, S["hidT"], DFF, L["w2"], 0, 512, 4, FS, NT, 256, pre, epi)
    kb.phase_end()


def emit_layer(kb, L, S, xsrc, NT, FS, own_tile, xmid, xfin, xfin_off):
    phase_mod(kb, L, S)
    phase_norm(kb, L, S, "norm1", xsrc, 0, NT, "g_mix", 1, 0, dstT=S["hT"])
    gemm_fm(kb, "pin", S["hT"], D, L["w_in"], L["b_inT"], NT, [
        (OFF_K + 1024, 512, 0, AF.Identity, 1.0, S["kT"][1024:, :]),
        (OFF_K, 1024, FS - 512, AF.Identity, 1.0, S["kT"]),
        (OFF_Q, DBH, FS, AF.Identity, QSCALE, S["qT"]),
        (OFF_U, DA, FS, AF.Gelu_apprx_tanh, 1.0, S["uT"]),
        (OFF_GA, 2 * D, FS, AF.Sigmoid, 1.0, S["sgT"]),
    ])
    phase_vproj(kb, L, S, FS, NT)
    phase_gating(kb, L, S, FS, NT)
    phase_attn(kb, L, S, FS, NT, own_tile)
    phase_merge(kb, L, S, FS, NT)
    phase_outproj(kb, L, S, FS, NT, xsrc, xmid)
    phase_norm(kb, L, S, "norm2", xmid, FS, NT, "g_mlp", 4, 3, dstT=S["hT"])
    gemm_fm(kb, "p1", S["hT"], D, L["w1"], L["b1T"], NT, [(0, DFF, FS, AF.Relu, 1.0, S["hidT"])], post_sq=True)
    phase_mlp2(kb, L, S, FS, NT, xmid, xfin, xfin_off)


LAYER_PARAMS = [("w_ada", [D, 6 * D]), ("b_ada", [6 * D]), ("g_mix", [D]), ("w_in", [D, INC]), ("b_in", [INC]),
                ("b_inT", [128, INC // 128]), ("g_v", [DA]), ("b_v", [DA]), ("w_s", [8, 128, 128]), ("b_s", [8, 128]),
                ("w_oa", [DA, D]), ("w_ob", [512, D]), ("w_out", [D, D]), ("g_mlp", [D]), ("w1", [D, DFF]),
                ("b1T", [128, DFF // 128]), ("w2", [DFF, D]), ("b2", [D])]


def build_program(layers, NT0, final, debug=False):
    nc = bass.Bass("TRN2", target_bir_lowering=False)
    xw = nc.dram_tensor("xw", [NT0, D], F32, kind="ExternalInput").ap()
    cT = nc.dram_tensor("cT", [128, 16], F32, kind="ExternalInput").ap()
    Ls = []
    for li in layers:
        L = {"cT": cT}
        for nm_, shp in LAYER_PARAMS:
            L[nm_] = nc.dram_tensor(f"{nm_}_{li}", shp, F32, kind="ExternalInput").ap()
        Ls.append(L)
    g_final = nc.dram_tensor("g_final", [D], F32, kind="ExternalInput").ap()
    c_ident = nc.dram_tensor("c_ident", [128, 128], F32, kind="ExternalInput").ap()
    c_tril = nc.dram_tensor("c_tril", [128, 128], F32, kind="ExternalInput").ap()
    c_nm = nc.dram_tensor("c_nm", [128, NMW_COLS], F32, kind="ExternalInput").ap()
    c_cb = nc.dram_tensor("c_cb", [len(layers), 128, 408], F32, kind="ExternalInput").ap()
    out = nc.dram_tensor("out", [OWN, D], F32, kind="ExternalOutput").ap()

    S = {}
    _dt = nc.dram_tensor
    if debug:
        class _W:
            def dram_tensor(self, *a, **k):
                return _dt(*a, kind="ExternalOutput", **k)
        ncd = _W()
    else:
        ncd = nc
    S["modbc"] = ncd.dram_tensor("s_modbc", [128, 6 * D], F32).ap()
    S["hT"] = ncd.dram_tensor("s_hT", [D, NT0], BF16).ap()
    S["uT"] = ncd.dram_tensor("s_uT", [DA, NT0], BF16).ap()
    S["qT"] = ncd.dram_tensor("s_qT", [DBH, NT0], BF16).ap()
    S["kT"] = ncd.dram_tensor("s_kT", [DBH, NT0], BF16).ap()
    S["sgT"] = ncd.dram_tensor("s_sgT", [2 * D, NT0], BF16).ap()
    S["V"] = ncd.dram_tensor("s_V", [NT0, DBH], BF16).ap()
    S["yaT"] = ncd.dram_tensor("s_yaT", [DA, NT0], BF16).ap()
    S["ybT"] = ncd.dram_tensor("s_ybT", [512, NT0], BF16).ap()
    S["mT"] = ncd.dram_tensor("s_mT", [D, NT0], BF16).ap()
    S["hidT"] = ncd.dram_tensor("s_hidT", [DFF, NT0], BF16).ap()
    xs = ncd.dram_tensor("s_xs", [NT0, D], F32).ap()

    kb = KB(nc)
    identf = kb.alloc("identf", [128, 128], F32, persistent=True)
    S["identb"] = kb.alloc("identb", [128, 128], BF16, persistent=True)
    S["tril"] = kb.alloc("tril", [128, 128], F32, persistent=True)
    S["nmw"] = kb.alloc("nmw", [128, NMW_COLS], F32, persistent=True)
    S["zerosb"] = kb.alloc("zerosb", [128, 128], BF16, persistent=True)
    S["onesb"] = kb.alloc("onesb", [128, 128], BF16, persistent=True)
    cbs = [kb.alloc(f"cb{i}", [128, 408], F32, persistent=True) for i in range(len(layers))]
    kb.phase_begin("consts")
    kb.op("sp", lambda e: e.dma_start(out=identf[:], in_=c_ident), w=["identf"], dma="ld_identf")
    kb.op("sp", lambda e: e.dma_start(out=S["tril"][:], in_=c_tril), w=["tril"], dma="ld_tril")
    kb.op("sp", lambda e: e.dma_start(out=S["nmw"][:], in_=c_nm), w=["nm"], dma="ld_nm")
    kb.op("pool", lambda e: e.memset(S["zerosb"][:], 0.0), w=["zerosb"])
    for i in range(len(layers)):
        kb.op("sp", lambda e, i=i: e.dma_start(out=cbs[i][:], in_=c_cb[i]), w=[f"cb{i}"], dma=f"ld_cb{i}")
    kb.op("dve", lambda e: e.tensor_copy(out=S["identb"][:], in_=identf[:]), r=["identf"], w=["identb"])
    kb.op("pool", lambda e: e.memset(S["onesb"][:], 1.0), w=["onesb"])
    kb.phase_end()

    nl = len(layers)
    if nl == 1:
        NT, FS = NT0, NT0 - OWN
        S["cb"] = cbs[0]
        if final:
            emit_layer(kb, Ls[0], S, xw, NT, FS, FS // 128, xs, xs, 0)
            phase_norm(kb, {"g_final": g_final}, S, "normf", xs, FS, NT, "g_final", None, None, dst_final=out, dst_off=FS)
        else:
            emit_layer(kb, Ls[0], S, xw, NT, FS, FS // 128, xs, out, FS)
    else:
        NT, FS = NT0, HALO
        S["cb"] = cbs[0]
        emit_layer(kb, Ls[0], S, xw, NT, FS, (NT0 - OWN) // 128, xs, xs, 0)
        S1 = dict(S)
        for k_ in ("hT", "uT", "qT", "kT", "sgT", "yaT", "ybT", "mT", "hidT"):
            S1[k_] = S[k_][:, HALO:]
        S1["V"] = S["V"][HALO:, :]
        S1["cb"] = cbs[1]
        x1 = xs[HALO:, :]
        NT1 = NT0 - HALO
        emit_layer(kb, Ls[1], S1, x1, NT1, NT1 - OWN, (NT1 - OWN) // 128, x1, x1, 0)
        phase_norm(kb, {"g_final": g_final}, S1, "normf", x1, NT1 - OWN, NT1, "g_final", None, None, dst_final=out, dst_off=NT1 - OWN)
    kb.finish()
    return nc


def _const_tables():
    ident = np.eye(128, dtype=np.float32)
    t = np.arange(128)
    tril = (t[None, :] <= t[:, None]).astype(np.float32)
    k = np.arange(128)[:, None]
    nm = np.zeros((128, NMW_COLS), np.float32)
    BIG = -1.0e7
    for g, d in enumerate((1, 4, 16)):
        ncol = (d + 1) * 128
        dist = np.arange(ncol)[None, :] - k
        valid = (dist >= 0) & (dist <= 128 * d) & (dist % d == 0)
        nm[:, NMW_OFF[g]:NMW_OFF[g] + ncol] = np.where(valid, -dist.astype(np.float32), BIG)
    return ident, tril, nm


def _cb_table(first_of_batch):
    cb = np.zeros((128, 408), np.float32)
    hb = -30000.0 if first_of_batch else 0.0
    for h in range(12):
        for dl in range(17):
            v = -SLOPES[h] * 128.0 * dl
            cb[:, (h * 17 + dl) * 2 + 0] = v
            cb[:, (h * 17 + dl) * 2 + 1] = v + hb
    return cb


def _layer_inputs(inputs, li):
    f = lambda a: np.ascontiguousarray(np.asarray(a, dtype=np.float32))
    d = {}
    for nm_ in ("w_ada", "b_ada", "g_mix", "w_in", "b_in", "g_v", "b_v", "w_s", "b_s", "w_oa", "w_ob", "w_out",
                "g_mlp", "w1", "w2", "b2"):
        d[f"{nm_}_{li}"] = f(inputs[nm_][li])
    d[f"b_inT_{li}"] = f(np.asarray(inputs["b_in"][li]).reshape(INC // 128, 128).T)
    d[f"b1T_{li}"] = f(np.asarray(inputs["b1"][li]).reshape(DFF // 128, 128).T)
    return d


_PROG_CACHE = {}


def _window(xb, t0, n_before):
    w = np.zeros((n_before + OWN, D), np.float32)
    lo = t0 - n_before
    if lo >= 0:
        w[:] = xb[lo:t0 + OWN]
    else:
        w[-lo:] = xb[0:t0 + OWN]
    return w


FUSED = True


def kernel(**inputs):
    x = np.asarray(inputs["x"], dtype=np.float32)
    c = np.asarray(inputs["c"], dtype=np.float32)
    ident, tril, nm = _const_tables()
    B, SEQ, _ = x.shape
    per_b = SEQ // OWN
    common = {"g_final": np.ascontiguousarray(np.asarray(inputs["g_final"], np.float32)),
              "c_ident": ident, "c_tril": tril, "c_nm": nm}

    def run(layers, NT0, final, xcur):
        key = (tuple(layers), NT0, final)
        if key not in _PROG_CACHE:
            _PROG_CACHE[key] = build_program(layers, NT0, final)
        nc = _PROG_CACHE[key]
        shared = dict(common)
        for li in layers:
            shared.update(_layer_inputs(inputs, li))
        in_maps = []
        for core in range(N_CORES):
            b, qd = core // per_b, core % per_b
            m = dict(shared)
            m["xw"] = _window(xcur[b], qd * OWN, NT0 - OWN)
            m["cT"] = np.ascontiguousarray(c[b].reshape(16, 128).T)
            m["c_cb"] = np.stack([_cb_table(qd == 0)] * len(layers))
            in_maps.append(m)
        res = run_bass_kernel_spmd(nc, in_maps, core_ids=list(range(N_CORES)))
        o = np.empty_like(xcur)
        for core in range(N_CORES):
            b, qd = core // per_b, core % per_b
            o[b, qd * OWN:(qd + 1) * OWN] = res.results[core]["out"]
        return o

    if FUSED:
        return run([0, 1], OWN + 2 * HALO, True, x)
    x1 = run([0], OWN + HALO, False, x)
    return run([1], OWN + HALO, True, x1)
```

```python
import math
from contextlib import ExitStack

import numpy as np
import concourse.bass as bass
import concourse.mybir as mybir
from concourse.bass_utils import run_bass_kernel_spmd

F32 = mybir.dt.float32
BF16 = mybir.dt.bfloat16
AF = mybir.ActivationFunctionType
ALU = mybir.AluOpType

D = 2048
DA = 1024
DBH = 1536
DFF = 8192
INC = 10752
OFF_U, OFF_VA, OFF_Q, OFF_K, OFF_V, OFF_GA, OFF_GB = 0, 1024, 2048, 3584, 5120, 6656, 8704
NDELTA = (1, 4, 16)
SLOPES = [2.0 ** (-8.0 * i / 12.0) for i in range(1, 13)]
EPS = 1e-6
QSCALE = 128.0 ** -0.5
N_CORES = 8
OWN = 4096
HALO = 2048


class KB:
    def __init__(self, nc):
        self.nc = nc
        self.E = dict(pe=nc.tensor, act=nc.scalar, dve=nc.vector, pool=nc.gpsimd, sp=nc.sync)
        self.gs = ExitStack()
        self.csem = {e: self.gs.enter_context(nc.semaphore(f"c_{e}")) for e in ("pe", "act", "dve", "pool")}
        self.ccnt = {e: 0 for e in self.csem}
        self.dsem = {}
        self.dcnt = {}
        self.dkey = {}
        self.dfree = []
        self.dfree_sw = []
        self.dsw = set()
        self.lastw = {}
        self.readers = {}
        self.waited = {e: {} for e in self.E}
        self.uid = 0
        self.ps = [self.gs.enter_context(nc.psum_tensor(f"psb{i}", [128, 512], F32)) for i in range(8)]
        self.psi = -1
        self.ph = None

    def psum(self):
        self.psi = (self.psi + 1) % 8
        return self.ps[self.psi], f"ps{self.psi}"

    def psum_at(self, i):
        return self.ps[i], f"ps{i}"

    def phase_begin(self, name):
        self.ph = ExitStack()
        self.pname = name

    def alloc(self, name, shape, dtype, persistent=False):
        self.uid += 1
        st = self.gs if persistent else self.ph
        return st.enter_context(self.nc.sbuf_tensor(f"{name}_{self.uid}", list(shape), dtype))

    def _dsem(self, key, eng):
        sw = eng == "pool"
        free = self.dfree_sw if sw else self.dfree
        if key not in self.dkey:
            if free:
                idx = free.pop()
            else:
                idx = len(self.dsem)
                self.dsem[idx] = self.gs.enter_context(self.nc.semaphore(f"d_{idx}"))
                self.dcnt[idx] = 0
                if sw:
                    self.dsw.add(idx)
            self.dkey[key] = idx
        assert (self.dkey[key] in self.dsw) == sw, key
        return self.dkey[key]

    def _wait(self, eng, ev):
        kind, key, val = ev
        w = self.waited[eng]
        if w.get((kind, key), 0) >= val:
            return
        if kind == "c" and key == "pe" and eng == "pe":
            return
        sem = self.csem[key] if kind == "c" else self.dsem[key]
        self.E[eng].wait_ge(sem, val)
        w[(kind, key)] = val

    def op(self, eng, fn, r=(), w=(), dma=None):
        evs = []
        for k in r:
            if k in self.lastw:
                evs.append(self.lastw[k])
        for k in w:
            if k in self.lastw:
                evs.append(self.lastw[k])
            evs.extend(self.readers.get(k, ()))
        for ev in evs:
            self._wait(eng, ev)
        ins = fn(self.E[eng])
        if dma is not None:
            idx = self._dsem(dma, eng)
            ins.then_inc(self.dsem[idx], 16)
            self.dcnt[idx] += 16
            ev = ("d", idx, self.dcnt[idx])
        else:
            ins.then_inc(self.csem[eng], 1)
            self.ccnt[eng] += 1
            ev = ("c", eng, self.ccnt[eng])
        for k in w:
            self.lastw[k] = ev
            self.readers[k] = []
        for k in r:
            if k not in w:
                self.readers.setdefault(k, []).append(ev)
        return ev

    def barrier(self):
        for eng in self.E:
            for e, c in self.ccnt.items():
                if c:
                    self._wait(eng, ("c", e, c))
            for k, c in self.dcnt.items():
                if c:
                    self._wait(eng, ("d", k, c))
        self.lastw = {}
        self.readers = {}
        for idx in self.dkey.values():
            (self.dfree_sw if idx in self.dsw else self.dfree).append(idx)
        self.dkey = {}

    def phase_end(self):
        self.barrier()
        self.ph.close()
        self.ph = None

    def finish(self):
        self.gs.close()


class Ring:
    def __init__(self, kb, name, n, shape, dtype):
        self.t = [kb.alloc(f"{name}{i}", shape, dtype) for i in range(n)]
        kb.uid += 1
        self.keys = [f"{name}{i}_{kb.uid}" for i in range(n)]
        self.i = -1
        self.n = n

    def next(self):
        self.i = (self.i + 1) % self.n
        return self.t[self.i], self.keys[self.i]


def mm_group(ps_ap, pairs):
    def fn(pe):
        ins = None
        n = len(pairs)
        for i, (a, b) in enumerate(pairs):
            ins = pe.matmul(ps_ap, lhsT=a, rhs=b, start=(i == 0), stop=(i == n - 1))
        return ins
    return fn


def load_bc(kb, dst, vec_ap, key):
    kb.op("sp", lambda e: e.dma_start(out=dst, in_=vec_ap.partition_broadcast(128)), w=[key], dma="ld_" + key)


def phase_mod(kb, L, S):
    kb.phase_begin("mod")
    cs = kb.alloc("cs", [128, 16], F32)
    crep = kb.alloc("crep", [128, 16, 128], F32)
    kb.op("sp", lambda e: e.dma_start(out=cs[:], in_=L["cT"]), w=["cs"], dma="ld_cs")
    kb.op("act", lambda e: e.activation(out=cs[:], in_=cs[:], func=AF.Silu), r=["cs"], w=["cs"])
    for k in range(16):
        kb.op("dve", lambda e, k=k: e.tensor_copy(out=crep[:, k, :], in_=cs[:, k:k + 1].to_broadcast([128, 128])),
              r=["cs"], w=[f"crep{k}"])
    wr = Ring(kb, "wada", 3, [128, 16, 512], F32)
    br = Ring(kb, "bada", 2, [128, 512], F32)
    orr = Ring(kb, "modo", 2, [128, 512], F32)
    wsrc = L["w_ada"].rearrange("(k p) c -> p k c", p=128)
    for cg in range(24):
        wt, wk = wr.next()
        bt, bk = br.next()
        ot, ok = orr.next()
        cs_ = slice(cg * 512, (cg + 1) * 512)
        kb.op("sp", lambda e: e.dma_start(out=wt[:], in_=wsrc[:, :, cs_]), w=[wk], dma="ld_" + wk)
        load_bc(kb, bt[:], L["b_ada"][cs_], bk)
        ps, pk = kb.psum()
        kb.op("pe", mm_group(ps[:], [(crep[:, k, :], wt[:, k, :]) for k in range(16)]),
              r=[wk] + [f"crep{k}" for k in range(16)], w=[pk])
        kb.op("dve", lambda e: e.tensor_tensor(out=ot[:], in0=ps[:], in1=bt[:], op=ALU.add), r=[pk, bk], w=[ok])
        kb.op("act", lambda e: e.dma_start(out=S["modbc"][:, cs_], in_=ot[:]), r=[ok], dma="st_" + ok)
    kb.phase_end()


def load_gm_sh(kb, L, S, gname, sc_idx, sh_idx):
    gm = kb.alloc("gm", [128, D], F32)
    gb = kb.alloc("gbc", [128, D], F32)
    load_bc(kb, gb[:], L[gname], "gbc")
    if sc_idx is not None:
        kb.op("sp", lambda e: e.dma_start(out=gm[:], in_=S["modbc"][:, sc_idx * D:(sc_idx + 1) * D]), w=["gm"], dma="ld_gm")
        kb.op("dve", lambda e: e.scalar_tensor_tensor(out=gm[:], in0=gm[:], scalar=1.0, in1=gb[:], op0=ALU.add, op1=ALU.mult),
              r=["gm", "gbc"], w=["gm"])
    else:
        kb.op("dve", lambda e: e.tensor_copy(out=gm[:], in_=gb[:]), r=["gbc"], w=["gm"])
    sh = None
    if sh_idx is not None:
        sh = kb.alloc("sh", [128, D], F32)
        kb.op("sp", lambda e: e.dma_start(out=sh[:], in_=S["modbc"][:, sh_idx * D:(sh_idx + 1) * D]), w=["sh"], dma="ld_sh")
    return gm, sh


def phase_norm(kb, L, S, name, src, t0, t1, gname, sc_idx, sh_idx, dstT=None, dst_final=None, dst_off=0):
    kb.phase_begin(name)
    gm, sh = load_gm_sh(kb, L, S, gname, sc_idx, sh_idx)
    ident = S["identb"]
    xr = Ring(kb, "nx", 8, [128, D], F32)
    jt = kb.alloc("nj", [128, D], BF16)
    sr = Ring(kb, "ns", 2, [128, 8], F32)
    tr = Ring(kb, "nt", 3, [128, D], F32)
    hr = Ring(kb, "nh", 3, [128, D], BF16)
    hTr = Ring(kb, "nhT", 2, [128, 16, 512], BF16)
    dT = dstT.rearrange("(k p) t -> p k t", p=128) if dstT is not None else None
    for ct in range(t0 // 512, t1 // 512):
        if dstT is not None:
            hT, hTk = hTr.next()
        st, sk = sr.next()
        xs_ = []
        for r4 in range(4):
            tok = ct * 512 + r4 * 128
            xt, xk = xr.next()
            xs_.append((xt, xk))
            kb.op("sp", lambda e: e.dma_start(out=xt[:], in_=src[tok:tok + 128, :]), w=[xk], dma="ld_" + xk)
            kb.op("act", lambda e: e.activation(out=jt[:], in_=xt[:], func=AF.Square, accum_out=st[:, r4:r4 + 1]),
                  r=[xk], w=["nj", sk])
        kb.op("act", lambda e: e.activation(out=st[:, 4:8], in_=st[:, 0:4], func=AF.Sqrt, bias=EPS, scale=1.0 / D), r=[sk], w=[sk])
        kb.op("dve", lambda e: e.reciprocal(out=st[:, 4:8], in_=st[:, 4:8]), r=[sk], w=[sk])
        for r4 in range(4):
            tok = ct * 512 + r4 * 128
            xt, xk = xs_[r4]
            tt, tk = tr.next()
            kb.op("dve", lambda e: e.scalar_tensor_tensor(out=tt[:], in0=xt[:], scalar=st[:, 4 + r4:5 + r4], in1=gm[:],
                                                           op0=ALU.mult, op1=ALU.mult), r=[xk, sk, "gm"], w=[tk])
            if dstT is None:
                kb.op("pool", lambda e: e.dma_start(out=dst_final[tok - dst_off:tok - dst_off + 128, :], in_=tt[:]),
                      r=[tk], dma="st_" + tk)
                continue
            hb, hk = hr.next()
            kb.op("pool", lambda e: e.tensor_tensor(out=hb[:, 0:1024], in0=tt[:, 0:1024], in1=sh[:, 0:1024], op=ALU.add), r=[tk, "sh"], w=[hk + "a"])
            kb.op("dve", lambda e: e.tensor_tensor(out=hb[:, 1024:2048], in0=tt[:, 1024:2048], in1=sh[:, 1024:2048], op=ALU.add), r=[tk, "sh"], w=[hk + "b"])
            for half in range(2):
                ps, pk = kb.psum()
                pv = ps[:].bitcast(BF16).rearrange("p (k t) -> p k t", t=128)

                def tfn(pe, half=half, pv=pv, hb=hb):
                    ins = None
                    for k in range(8):
                        kk = half * 8 + k
                        ins = pe.transpose(pv[:, k, :], hb[:, kk * 128:(kk + 1) * 128], ident[:])
                    return ins
                kb.op("pe", tfn, r=[hk + ("a" if half == 0 else "b")], w=[pk])
                dst = hT[:, half * 8:(half + 1) * 8, r4 * 128:(r4 + 1) * 128]
                if half == 0:
                    kb.op("act", lambda e, dst=dst, pv=pv: e.copy(out=dst, in_=pv), r=[pk], w=[hTk])
                else:
                    kb.op("dve", lambda e, dst=dst, pv=pv: e.tensor_copy(out=dst, in_=pv), r=[pk], w=[hTk])
        if dstT is not None:
            kb.op("act", lambda e: e.dma_start(out=dT[:, :, ct * 512:(ct + 1) * 512], in_=hT[:]), r=[hTk], dma="st_" + hTk)
    kb.phase_end()


def gemm_fm(kb, name, inT, K, W, biasT, tok1, jobs, post_sq=False):
    kb.phase_begin(name)
    KC = K // 128
    CG = 1024
    wr = Ring(kb, "fw", 2, [128, KC, CG], BF16)
    ar = Ring(kb, "fa", 2, [128, KC, 512], BF16)
    orr = Ring(kb, "fo", 2, [128, 8, 512], BF16)
    br = Ring(kb, "fb", 2, [128, 8], F32)
    rr = Ring(kb, "fr", 2, [128, 512], F32) if post_sq else None
    wsrc = W.rearrange("(k p) c -> p k c", p=128)
    isrc = inT.rearrange("(k p) t -> p k t", p=128)
    stq = "pool" if post_sq else "act"
    groups = []
    for (c0, ncols, tok0, func, scale, outT) in jobs:
        odst = outT.rearrange("(c p) t -> p c t", p=128)
        for cg in range((ncols + CG - 1) // CG):
            groups.append((c0 + cg * CG, min(CG, ncols - cg * CG), tok0, func, scale, odst, (cg * CG) // 128))

    def load_group(gi):
        cc0, ncg, tok0, func, scale, odst, r0 = groups[gi]
        nch = ncg // 128
        wt, wk = wr.next()
        bt, bk = br.next()
        kb.op("pool", lambda e: e.dma_start(out=wt[:, :, 0:ncg], in_=wsrc[:, :, cc0:cc0 + ncg]), w=[wk], dma="ld_" + wk)
        kb.op("sp", lambda e: e.dma_start(out=bt[:, 0:nch], in_=biasT[:, cc0 // 128:cc0 // 128 + nch]), w=[bk], dma="ld_" + bk)
        if scale != 1.0:
            kb.op("dve", lambda e: e.tensor_scalar(out=bt[:, 0:nch], in0=bt[:, 0:nch], scalar1=float(scale), scalar2=None,
                                                    op0=ALU.mult), r=[bk], w=[bk])
        return wt, wk, bt, bk, nch

    nxt = load_group(0)
    for gi in range(len(groups)):
        wt, wk, bt, bk, nch = nxt
        cc0, ncg, tok0, func, scale, odst, r0 = groups[gi]
        for ct in range(tok0 // 512, tok1 // 512):
            at, ak = ar.next()
            ot, ok = orr.next()
            kb.op("sp", lambda e: e.dma_start(out=at[:], in_=isrc[:, :, ct * 512:(ct + 1) * 512]), w=[ak], dma="ld_" + ak)
            for cc in range(nch):
                ps, pk = kb.psum()
                kb.op("pe", mm_group(ps[:], [(wt[:, k, cc * 128:(cc + 1) * 128], at[:, k, :]) for k in range(KC)]),
                      r=[wk, ak], w=[pk])
                if post_sq:
                    rt, rk = rr.next()
                    kb.op("act", lambda e: e.activation(out=rt[:], in_=ps[:], func=func, bias=bt[:, cc:cc + 1], scale=float(scale)),
                          r=[pk, bk], w=[rk])
                    kb.op("pool", lambda e: e.tensor_tensor(out=ot[:, cc, :], in0=rt[:], in1=rt[:], op=ALU.mult), r=[rk], w=[ok])
                else:
                    kb.op("act", lambda e: e.activation(out=ot[:, cc, :], in_=ps[:], func=func, bias=bt[:, cc:cc + 1], scale=float(scale)),
                          r=[pk, bk], w=[ok])
            kb.op(stq, lambda e: e.dma_start(out=odst[:, r0:r0 + nch, ct * 512:(ct + 1) * 512], in_=ot[:, 0:nch, :]),
                  r=[ok], dma="st_" + ok)
            if ct == tok0 // 512 and gi + 1 < len(groups):
                nxt = load_group(gi + 1)
    kb.phase_end()


def gemm_tm(kb, inT, K, W, c0, CGc, ngroups, tok0, tok1, TT, pre, epi, nbuf=2):
    KC = K // 128
    NQ = 4
    KQ = KC // NQ
    wt = kb.alloc("tw", [128, KC, CGc], BF16)
    ar = Ring(kb, "ta", nbuf, [128, KC, TT], BF16)
    wsrc = W.rearrange("(k p) c -> p k c", p=128)
    isrc = inT.rearrange("(k p) t -> p k t", p=128)
    for cg in range(ngroups):
        cc0 = c0 + cg * CGc
        for qi in range(NQ):
            ks = slice(qi * KQ, (qi + 1) * KQ)
            kb.op("pool", lambda e, ks=ks: e.dma_start(out=wt[:, ks, :], in_=wsrc[:, ks, cc0:cc0 + CGc]), w=[f"tw{qi}"], dma=f"ld_tw{qi}")
        for tt in range(tok0 // TT, tok1 // TT):
            at, ak = ar.next()
            kb.op("sp", lambda e: e.dma_start(out=at[:], in_=isrc[:, :, tt * TT:(tt + 1) * TT]), w=[ak], dma="ld_" + ak)
            for r in range(TT // 128):
                tok = tt * TT + r * 128
                if pre is not None:
                    pre(tok, cg)
                for cb in range(CGc // 512):
                    ps, pk = kb.psum()
                    for qi in range(NQ):
                        def fn(pe, qi=qi, ps=ps, at=at, r=r, cb=cb):
                            ins = None
                            for k in range(qi * KQ, (qi + 1) * KQ):
                                ins = pe.matmul(ps[:], lhsT=at[:, k, r * 128:(r + 1) * 128], rhs=wt[:, k, cb * 512:(cb + 1) * 512],
                                                start=(k == 0), stop=(k == KC - 1))
                            return ins
                        kb.op("pe", fn, r=[f"tw{qi}", ak], w=[pk])
                    epi(tok, cg * (CGc // 512) + cb, ps, pk)


def phase_vproj(kb, L, S, FS, NT):
    kb.phase_begin("vproj")
    bvb = kb.alloc("bvb", [128, DBH], F32)
    load_bc(kb, bvb[:], L["b_in"][OFF_V:OFF_V + DBH], "bvb")
    vr = Ring(kb, "vt", 2, [128, DBH], BF16)
    cur = {}

    def pre(tok, cg):
        cur["t"], cur["k"] = vr.next()

    def epi_a(tok, cb, ps, pk):
        vt, vk = cur["t"], cur["k"]
        kb.op("dve", lambda e: e.tensor_tensor(out=vt[:, 1024:1536], in0=ps[:], in1=bvb[:, 1024:1536], op=ALU.add),
              r=[pk, "bvb"], w=[vk])
        kb.op("act", lambda e: e.dma_start(out=S["V"][tok:tok + 128, 1024:1536], in_=vt[:, 1024:1536]), r=[vk], dma="st_" + vk)

    def epi_b(tok, cb, ps, pk):
        vt, vk = cur["t"], cur["k"]
        kb.op("dve", lambda e: e.tensor_tensor(out=vt[:, cb * 512:(cb + 1) * 512], in0=ps[:], in1=bvb[:, cb * 512:(cb + 1) * 512], op=ALU.add),
              r=[pk, "bvb"], w=[vk])
        if cb == 2:
            kb.op("act", lambda e: e.dma_start(out=S["V"][tok:tok + 128, :], in_=vt[:]), r=[vk], dma="st_" + vk)

    gemm_tm(kb, S["hT"], D, L["w_in"], OFF_V + 1024, 512, 1, 0, FS - 512, 512, pre, epi_a)
    gemm_tm(kb, S["hT"], D, L["w_in"], OFF_V, DBH, 1, FS - 512, NT, 512, pre, epi_b)
    kb.phase_end()


def phase_gating(kb, L, S, FS, NT):
    kb.phase_begin("gating")
    ident = S["identb"]
    wsn = kb.alloc("wsn", [128, 8, 128], F32)
    wsb = kb.alloc("wsb", [128, 8, 128], BF16)
    wmT = kb.alloc("wmT", [128, 8, 128], BF16)
    kb.op("sp", lambda e: e.dma_start(out=wsn[:], in_=L["w_s"].rearrange("g t s -> t g s")), w=["wsn"], dma="ld_wsn")
    kb.op("dve", lambda e: e.tensor_tensor(out=wsb[:], in0=wsn[:], in1=S["tril"][:].unsqueeze(1).to_broadcast([128, 8, 128]), op=ALU.mult),
          r=["wsn"], w=["wsb"])
    ps, pk = kb.psum()
    pv = ps[:].bitcast(BF16).rearrange("p (k t) -> p k t", t=128)

    def tfn(pe):
        ins = None
        for g in range(8):
            ins = pe.transpose(pv[:, g, :], wsb[:, g, :], ident[:])
        return ins
    kb.op("pe", tfn, r=["wsb"], w=[pk])
    kb.op("dve", lambda e: e.tensor_copy(out=wmT[:], in_=pv), r=[pk], w=["wmT"])
    bsb = kb.alloc("bsb", [128, 8, 128], F32)
    load_bc(kb, bsb[:].rearrange("p g t -> p (g t)"), L["b_s"].rearrange("g t -> (g t)"), "bsb")
    bvab = kb.alloc("bvab", [128, DA], F32)
    load_bc(kb, bvab[:], L["b_in"][OFF_VA:OFF_VA + DA], "bvab")
    gvb = kb.alloc("gvb", [128, DA], F32)
    load_bc(kb, gvb[:], L["g_v"], "gvb")
    bvb2 = kb.alloc("bvb2", [128, DA], F32)
    load_bc(kb, bvb2[:], L["b_v"], "bvb2")
    ur = Ring(kb, "gu", 2, [128, 8, 512], BF16)
    yr = Ring(kb, "gy", 2, [128, 8, 512], BF16)
    t1r = Ring(kb, "gt1", 2, [128, DA], F32)
    t2r = Ring(kb, "gt2", 2, [128, DA], F32)
    vnr = Ring(kb, "gvn", 2, [128, DA], BF16)
    str_ = Ring(kb, "gst", 2, [128, 16], F32)
    jt = kb.alloc("gjunk", [128, DA], BF16)
    usrc = S["uT"].rearrange("(g p) t -> p g t", p=128)
    ydst = S["yaT"].rearrange("(g p) t -> p g t", p=128)
    cur = {}

    def pre(tok, cg):
        r4 = (tok // 128) % 4
        if r4 == 0:
            ut, uk = ur.next()
            yt, yk = yr.next()
            cur.update(ut=ut, uk=uk, yt=yt, yk=yk)
            ct = tok // 512
            kb.op("sp", lambda e: e.dma_start(out=ut[:], in_=usrc[:, :, ct * 512:(ct + 1) * 512]), w=[uk], dma="ld_" + uk)
        t1, t1k = t1r.next()
        cur.update(t1=t1, t1k=t1k)

    def epi(tok, cb, ps, pk):
        t1, t1k = cur["t1"], cur["t1k"]
        kb.op("dve", lambda e: e.tensor_tensor(out=t1[:, cb * 512:(cb + 1) * 512], in0=ps[:], in1=bvab[:, cb * 512:(cb + 1) * 512], op=ALU.add),
              r=[pk, "bvab"], w=[t1k])
        if cb != 1:
            return
        r4 = (tok // 128) % 4
        ut, uk, yt, yk = cur["ut"], cur["uk"], cur["yt"], cur["yk"]
        t2, t2k = t2r.next()
        vn, vnk = vnr.next()
        st, sk = str_.next()
        kb.op("act", lambda e: e.activation(out=t1[:], in_=t1[:], func=AF.Gelu_apprx_tanh, accum_out=st[:, 0:1]), r=[t1k], w=[t1k, sk])

        kb.op("act", lambda e: e.activation(out=jt[:], in_=t1[:], func=AF.Square, accum_out=st[:, 1:2]), r=[t1k], w=[sk, "gjunk"])
        kb.op("dve", lambda e: e.tensor_scalar(out=st[:, 12:13], in0=st[:, 0:1], scalar1=1.0 / DA, scalar2=None, op0=ALU.mult), r=[sk], w=[sk])
        kb.op("dve", lambda e: e.tensor_tensor(out=st[:, 3:4], in0=st[:, 12:13], in1=st[:, 12:13], op=ALU.mult), r=[sk], w=[sk])
        kb.op("dve", lambda e: e.scalar_tensor_tensor(out=st[:, 13:14], in0=st[:, 1:2], scalar=1.0 / DA, in1=st[:, 3:4],
                                                       op0=ALU.mult, op1=ALU.subtract), r=[sk], w=[sk])
        kb.op("act", lambda e: e.activation(out=st[:, 14:15], in_=st[:, 13:14], func=AF.Sqrt, bias=EPS, scale=1.0), r=[sk], w=[sk])
        kb.op("dve", lambda e: e.reciprocal(out=st[:, 14:15], in_=st[:, 14:15]), r=[sk], w=[sk])
        kb.op("dve", lambda e: e.scalar_tensor_tensor(out=t2[:], in0=t1[:], scalar=st[:, 12:13], in1=gvb[:], op0=ALU.subtract, op1=ALU.mult),
              r=[t1k, sk, "gvb"], w=[t2k])
        kb.op("dve", lambda e: e.scalar_tensor_tensor(out=vn[:], in0=t2[:], scalar=st[:, 14:15], in1=bvb2[:], op0=ALU.mult, op1=ALU.add),
              r=[t2k, sk, "bvb2"], w=[vnk])
        for half in range(2):
            psg, pgk = kb.psum()
            pg = psg[:].rearrange("p (g t) -> p g t", t=128)

            def gfn(pe, half=half, pg=pg, vn=vn):
                ins = None
                for gi in range(4):
                    g = half * 4 + gi
                    ins = pe.matmul(pg[:, gi, :], lhsT=vn[:, g * 128:(g + 1) * 128], rhs=wmT[:, g, :], start=True, stop=True)
                return ins
            kb.op("pe", gfn, r=[vnk, "wmT"], w=[pgk])
            gs = slice(half * 4, half * 4 + 4)
            tsl = slice(r4 * 128, (r4 + 1) * 128)
            t2v = t2[:, half * 512:(half + 1) * 512].rearrange("p (g t) -> p g t", t=128)
            kb.op("dve", lambda e, pg=pg, gs=gs, t2v=t2v: e.tensor_tensor(out=t2v, in0=pg, in1=bsb[:, gs, :], op=ALU.add),
                  r=[pgk, "bsb", vnk], w=[t2k])
            kb.op("pool", lambda e, gs=gs, tsl=tsl, t2v=t2v: e.tensor_tensor(out=yt[:, gs, tsl], in0=t2v, in1=ut[:, gs, tsl], op=ALU.mult),
                  r=[t2k, uk], w=[yk])
        if r4 == 3:
            ct = tok // 512
            kb.op("pool", lambda e: e.dma_start(out=ydst[:, :, ct * 512:(ct + 1) * 512], in_=yt[:]), r=[yk], dma="st_" + yk)

    gemm_tm(kb, S["hT"], D, L["w_in"], OFF_VA, DA, 1, FS, NT, 512, pre, epi)
    kb.phase_end()


NMW_OFF = (0, 256, 256 + 640)
NMW_COLS = 256 + 640 + 2176


def phase_attn(kb, L, S, FS, NT, own_tile):
    kb.phase_begin("attn")
    nmw = S["nmw"]
    cb = S["cb"]
    ones = S["onesb"]
    zeros = S["zerosb"]
    qr = Ring(kb, "aq", 2, [128, 3, 512], BF16)
    kr = Ring(kb, "ak", 2, [128, 3, 20 * 128], BF16)
    vr = Ring(kb, "av", 2, [128, 3, 20, 128], BF16)
    tr = Ring(kb, "at", 4, [128, 512], F32)
    pr = Ring(kb, "ap", 6, [128, 512], BF16)
    rr = Ring(kb, "ar", 2, [128, 512], F32)
    yr = Ring(kb, "ay", 2, [128, 4, 512], BF16)
    qsrc = S["qT"].rearrange("(h p) t -> p h t", p=128)
    ksrc = S["kT"].rearrange("(h p) t -> p h t", p=128)
    vsrc = S["V"].rearrange("(n p) (h c) -> p n h c", p=128, c=128)
    ydst = S["ybT"].rearrange("(j p) t -> p j t", p=128)
    acc_i = 0
    s_i = 0
    for st_ in range(FS // 512, NT // 512):
        yt, yk = yr.next()
        q0 = st_ * 4
        for j in range(4):
            qt, qk = qr.next()
            kt, kk_ = kr.next()
            vt, vk = vr.next()
            for g in range(3):
                h = g * 4 + j
                nd = NDELTA[g]
                k0 = q0 - nd
                nkt = nd + 4
                kb.op("sp", lambda e, g=g, h=h: e.dma_start(out=qt[:, g, :], in_=qsrc[:, h, st_ * 512:(st_ + 1) * 512]),
                      w=[qk], dma="ld_" + qk)
                kb.op("sp", lambda e, g=g, h=h, k0=k0, nkt=nkt: e.dma_start(out=kt[:, g, 0:nkt * 128], in_=ksrc[:, h, k0 * 128:(k0 + nkt) * 128]),
                      w=[kk_], dma="ld_" + kk_)
                kb.op("sp", lambda e, g=g, h=h, k0=k0, nkt=nkt: e.dma_start(out=vt[:, g, 0:nkt, :], in_=vsrc[:, k0:k0 + nkt, h, :]),
                      w=[vk], dma="ld_" + vk)
            acc_i = (acc_i + 1) % 2
            psn, pnk = kb.psum_at(2 * acc_i)
            psd, pdk = kb.psum_at(2 * acc_i + 1)
            kb.op("pe", lambda pe: pe.matmul(psn[:], lhsT=zeros[:], rhs=qt[:, 0, :], start=True, stop=False), r=[qk], w=[pnk])
            kb.op("pe", lambda pe: pe.matmul(psd[:], lhsT=zeros[:], rhs=qt[:, 0, :], start=True, stop=False), r=[qk], w=[pdk])
            steps = []
            for g in range(3):
                for kl in range(NDELTA[g] + 4):
                    steps.append((g, kl))
            nsteps = len(steps)
            pend = []
            done = 0

            def emit_pv(item, last):
                g, kl, qa, nq, pt, pkk = item
                cs = slice(qa * 128, (qa + nq) * 128)
                kb.op("pe", lambda pe: pe.matmul(psn[:, cs], lhsT=vt[:, g, kl, :], rhs=pt[:, 0:nq * 128], start=False, stop=last),
                      r=[pkk, vk], w=[pnk])
                kb.op("pe", lambda pe: pe.matmul(psd[:, cs], lhsT=ones[:], rhs=pt[:, 0:nq * 128], start=False, stop=last),
                      r=[pkk], w=[pdk])

            for (g, kl) in steps:
                h = g * 4 + j
                nd = NDELTA[g]
                qa = max(0, kl - nd)
                qb = min(3, kl)
                nq = qb - qa + 1
                w_ = nq * 128
                c0 = NMW_OFF[g] + (qa + nd - kl) * 128
                ki = q0 - nd + kl
                bcol = 1 if ki < own_tile else 0
                s_i = (s_i + 1) % 4
                pss, psk = kb.psum_at(4 + s_i)
                tt, tk = tr.next()
                pt, pkk = pr.next()
                kb.op("pe", lambda pe, g=g, kl=kl, qa=qa, w_=w_, pss=pss: pe.matmul(
                    pss[:, 0:w_], lhsT=kt[:, g, kl * 128:(kl + 1) * 128], rhs=qt[:, g, qa * 128:qa * 128 + w_], start=True, stop=True),
                    r=[kk_, qk], w=[psk])
                kb.op("dve", lambda e, c0=c0, w_=w_, h=h, pss=pss, tt=tt: e.scalar_tensor_tensor(
                    out=tt[:, 0:w_], in0=nmw[:, c0:c0 + w_], scalar=float(SLOPES[h]), in1=pss[:, 0:w_], op0=ALU.mult, op1=ALU.add),
                    r=[psk], w=[tk])
                kb.op("act", lambda e, bcol=bcol, w_=w_, tt=tt, pt=pt: e.activation(
                    out=pt[:, 0:w_], in_=tt[:, 0:w_], func=AF.Exp, bias=cb[:, bcol:bcol + 1], scale=1.0), r=[tk], w=[pkk])
                pend.append((g, kl, qa, nq, pt, pkk))
                if len(pend) > 3:
                    emit_pv(pend.pop(0), False)
                    done += 1
            while pend:
                done += 1
                emit_pv(pend.pop(0), done == nsteps)
            rt, rk = rr.next()
            kb.op("dve", lambda e: e.tensor_scalar(out=rt[:], in0=psd[:], scalar1=1e-30, scalar2=None, op0=ALU.add), r=[pdk], w=[rk])
            kb.op("dve", lambda e: e.reciprocal(out=rt[:], in_=rt[:]), r=[rk], w=[rk])
            kb.op("dve", lambda e: e.tensor_tensor(out=yt[:, j, :], in0=psn[:], in1=rt[:], op=ALU.mult), r=[pnk, rk], w=[yk])
        kb.op("pool", lambda e: e.dma_start(out=ydst[:, :, st_ * 512:(st_ + 1) * 512], in_=yt[:]), r=[yk], dma="st_" + yk)
    kb.phase_end()


def phase_merge(kb, L, S, FS, NT):
    kb.phase_begin("merge")
    woa = kb.alloc("woa", [128, 8, D], BF16)
    wob = kb.alloc("wob", [128, 4, D], BF16)
    kb.op("pool", lambda e: e.dma_start(out=woa[:], in_=L["w_oa"].rearrange("(k p) c -> p k c", p=128)), w=["woa"], dma="ld_woa")
    kb.op("pool", lambda e: e.dma_start(out=wob[:], in_=L["w_ob"].rearrange("(k p) c -> p k c", p=128)), w=["wob"], dma="ld_wob")
    yar = Ring(kb, "mya", 2, [128, 8, 512], BF16)
    ybr = Ring(kb, "myb", 2, [128, 4, 512], BF16)
    sar = Ring(kb, "msa", 2, [128, 4, 512], BF16)
    sbr = Ring(kb, "msb", 2, [128, 4, 512], BF16)
    t1r = Ring(kb, "mt1", 2, [128, 512], F32)
    t2r = Ring(kb, "mt2", 2, [128, 512], F32)
    mr = Ring(kb, "mm", 2, [128, 16, 512], BF16)
    yasrc = S["yaT"].rearrange("(k p) t -> p k t", p=128)
    ybsrc = S["ybT"].rearrange("(k p) t -> p k t", p=128)
    sgsrc = S["sgT"].rearrange("(k p) t -> p k t", p=128)
    mdst = S["mT"].rearrange("(k p) t -> p k t", p=128)
    for ct in range(FS // 512, NT // 512):
        ts = slice(ct * 512, (ct + 1) * 512)
        ya, yak = yar.next()
        yb, ybk = ybr.next()
        mt, mk = mr.next()
        kb.op("sp", lambda e: e.dma_start(out=ya[:], in_=yasrc[:, :, ts]), w=[yak], dma="ld_" + yak)
        kb.op("sp", lambda e: e.dma_start(out=yb[:], in_=ybsrc[:, :, ts]), w=[ybk], dma="ld_" + ybk)
        for fg in range(4):
            sa, sak = sar.next()
            sb, sbk = sbr.next()
            kb.op("sp", lambda e: e.dma_start(out=sa[:], in_=sgsrc[:, fg * 4:fg * 4 + 4, ts]), w=[sak], dma="ld_" + sak)
            kb.op("sp", lambda e: e.dma_start(out=sb[:], in_=sgsrc[:, 16 + fg * 4:16 + fg * 4 + 4, ts]), w=[sbk], dma="ld_" + sbk)
            for fc in range(4):
                f = fg * 4 + fc
                fs = slice(f * 128, (f + 1) * 128)
                psa, pak = kb.psum()
                psb, pbk = kb.psum()
                t1, t1k = t1r.next()
                t2, t2k = t2r.next()
                kb.op("pe", mm_group(psa[:], [(woa[:, k, fs], ya[:, k, :]) for k in range(8)]), r=["woa", yak], w=[pak])
                kb.op("pe", mm_group(psb[:], [(wob[:, k, fs], yb[:, k, :]) for k in range(4)]), r=["wob", ybk], w=[pbk])
                kb.op("dve", lambda e, fc=fc: e.tensor_tensor(out=t1[:], in0=psa[:], in1=sa[:, fc, :], op=ALU.mult), r=[pak, sak], w=[t1k])
                kb.op("dve", lambda e, fc=fc: e.tensor_tensor(out=t2[:], in0=psb[:], in1=sb[:, fc, :], op=ALU.mult), r=[pbk, sbk], w=[t2k])
                kb.op("pool", lambda e, f=f: e.tensor_tensor(out=mt[:, f, :], in0=t1[:], in1=t2[:], op=ALU.add), r=[t1k, t2k], w=[mk])
        kb.op("pool", lambda e: e.dma_start(out=mdst[:, :, ts], in_=mt[:]), r=[mk], dma="st_" + mk)
    kb.phase_end()


def phase_outproj(kb, L, S, FS, NT, xsrc, xmid):
    kb.phase_begin("outproj")
    gt = kb.alloc("gt1", [128, D], F32)
    kb.op("sp", lambda e: e.dma_start(out=gt[:], in_=S["modbc"][:, 2 * D:3 * D]), w=["gt1"], dma="ld_gt1")
    xr = Ring(kb, "ox", 2, [128, D], F32)
    tr = Ring(kb, "ot", 2, [128, 512], F32)
    cur = {}

    def pre(tok, cg):
        xt, xk = xr.next()
        cur.update(xt=xt, xk=xk)
        kb.op("sp", lambda e: e.dma_start(out=xt[:], in_=xsrc[tok:tok + 128, :]), w=[xk], dma="ld_" + xk)

    def epi(tok, cb, ps, pk):
        xt, xk = cur["xt"], cur["xk"]
        tt, tk = tr.next()
        cs = slice(cb * 512, (cb + 1) * 512)
        kb.op("dve", lambda e: e.tensor_tensor(out=tt[:], in0=ps[:], in1=gt[:, cs], op=ALU.mult), r=[pk, "gt1"], w=[tk])
        kb.op("pool", lambda e: e.tensor_tensor(out=xt[:, cs], in0=xt[:, cs], in1=tt[:], op=ALU.add), r=[tk, xk], w=[xk])
        if cb == 3:
            kb.op("pool", lambda e: e.dma_start(out=xmid[tok:tok + 128, :], in_=xt[:]), r=[xk], dma="st_" + xk)

    gemm_tm(kb, S["mT"], D, L["w_out"], 0, D, 1, FS, NT, 512, pre, epi)
    kb.phase_end()


def phase_mlp2(kb, L, S, FS, NT, xmid, xfin, xfin_off):
    kb.phase_begin("mlp2")
    gt = kb.alloc("gt2", [128, D], F32)
    b2 = kb.alloc("b2b", [128, D], F32)
    kb.op("sp", lambda e: e.dma_start(out=gt[:], in_=S["modbc"][:, 5 * D:6 * D]), w=["gt2"], dma="ld_gt2")
    load_bc(kb, b2[:], L["b2"], "b2b")
    xr = Ring(kb, "hx", 3, [128, 512], F32)
    tr = Ring(kb, "ht", 2, [128, 512], F32)
    cur = {}

    def pre(tok, cg):
        xt, xk = xr.next()
        cur.update(xt=xt, xk=xk)
        kb.op("sp", lambda e: e.dma_start(out=xt[:], in_=xmid[tok:tok + 128, cg * 512:(cg + 1) * 512]), w=[xk], dma="ld_" + xk)

    def epi(tok, cb, ps, pk):
        xt, xk = cur["xt"], cur["xk"]
        tt, tk = tr.next()
        cs = slice(cb * 512, (cb + 1) * 512)
        kb.op("dve", lambda e: e.tensor_tensor(out=tt[:], in0=ps[:], in1=b2[:, cs], op=ALU.add), r=[pk, "b2b"], w=[tk])
        kb.op("pool", lambda e: e.tensor_tensor(out=tt[:], in0=tt[:], in1=gt[:, cs], op=ALU.mult), r=[tk, "gt2"], w=[tk])
        kb.op("dve", lambda e: e.tensor_tensor(out=xt[:], in0=xt[:], in1=tt[:], op=ALU.add), r=[tk, xk], w=[xk])
        kb.op("act", lambda e: e.dma_start(out=xfin[tok - xfin_off:tok - xfin_off + 128, cs], in_=xt[:]), r=[xk], dma="st_" + xk)

    gemm_tm(kb, S["hidT"], DFF, L["w2"], 0, 512, 4, FS, NT, 256, pre, epi, nbuf=3)
    kb.phase_end()


def emit_layer(kb, L, S, xsrc, NT, FS, own_tile, xmid, xfin, xfin_off):
    phase_mod(kb, L, S)
    phase_norm(kb, L, S, "norm1", xsrc, 0, NT, "g_mix", 1, 0, dstT=S["hT"])
    gemm_fm(kb, "pin", S["hT"], D, L["w_in"], L["b_inT"], NT, [
        (OFF_K + 1024, 512, 0, AF.Identity, 1.0, S["kT"][1024:, :]),
        (OFF_K, 1024, FS - 512, AF.Identity, 1.0, S["kT"]),
        (OFF_Q, DBH, FS, AF.Identity, QSCALE, S["qT"]),
        (OFF_U, DA, FS, AF.Gelu_apprx_tanh, 1.0, S["uT"]),
        (OFF_GA, 2 * D, FS, AF.Sigmoid, 1.0, S["sgT"]),
    ])
    phase_vproj(kb, L, S, FS, NT)
    phase_gating(kb, L, S, FS, NT)
    phase_attn(kb, L, S, FS, NT, own_tile)
    phase_merge(kb, L, S, FS, NT)
    phase_outproj(kb, L, S, FS, NT, xsrc, xmid)
    phase_norm(kb, L, S, "norm2", xmid, FS, NT, "g_mlp", 4, 3, dstT=S["hT"])
    gemm_fm(kb, "p1", S["hT"], D, L["w1"], L["b1T"], NT, [(0, DFF, FS, AF.Relu, 1.0, S["hidT"])], post_sq=True)
    phase_mlp2(kb, L, S, FS, NT, xmid, xfin, xfin_off)


LAYER_PARAMS = [("w_ada", [D, 6 * D]), ("b_ada", [6 * D]), ("g_mix", [D]), ("w_in", [D, INC]), ("b_in", [INC]),
                ("b_inT", [128, INC // 128]), ("g_v", [DA]), ("b_v", [DA]), ("w_s", [8, 128, 128]), ("b_s", [8, 128]),
                ("w_oa", [DA, D]), ("w_ob", [512, D]), ("w_out", [D, D]), ("g_mlp", [D]), ("w1", [D, DFF]),
                ("b1T", [128, DFF // 128]), ("w2", [DFF, D]), ("b2", [D])]


def build_program(layers, NT0, final, debug=False):
    nc = bass.Bass("TRN2", target_bir_lowering=False)
    xw = nc.dram_tensor("xw", [NT0, D], F32, kind="ExternalInput").ap()
    cT = nc.dram_tensor("cT", [128, 16], F32, kind="ExternalInput").ap()
    Ls = []
    for li in layers:
        L = {"cT": cT}
        for nm_, shp in LAYER_PARAMS:
            L[nm_] = nc.dram_tensor(f"{nm_}_{li}", shp, F32, kind="ExternalInput").ap()
        Ls.append(L)
    g_final = nc.dram_tensor("g_final", [D], F32, kind="ExternalInput").ap()
    c_ident = nc.dram_tensor("c_ident", [128, 128], F32, kind="ExternalInput").ap()
    c_tril = nc.dram_tensor("c_tril", [128, 128], F32, kind="ExternalInput").ap()
    c_nm = nc.dram_tensor("c_nm", [128, NMW_COLS], F32, kind="ExternalInput").ap()
    c_cb = nc.dram_tensor("c_cb", [len(layers), 128, 408], F32, kind="ExternalInput").ap()
    out = nc.dram_tensor("out", [OWN, D], F32, kind="ExternalOutput").ap()

    S = {}
    _dt = nc.dram_tensor
    if debug:
        class _W:
            def dram_tensor(self, *a, **k):
                return _dt(*a, kind="ExternalOutput", **k)
        ncd = _W()
    else:
        ncd = nc
    S["modbc"] = ncd.dram_tensor("s_modbc", [128, 6 * D], F32).ap()
    S["hT"] = ncd.dram_tensor("s_hT", [D, NT0], BF16).ap()
    S["uT"] = ncd.dram_tensor("s_uT", [DA, NT0], BF16).ap()
    S["qT"] = ncd.dram_tensor("s_qT", [DBH, NT0], BF16).ap()
    S["kT"] = ncd.dram_tensor("s_kT", [DBH, NT0], BF16).ap()
    S["sgT"] = ncd.dram_tensor("s_sgT", [2 * D, NT0], BF16).ap()
    S["V"] = ncd.dram_tensor("s_V", [NT0, DBH], BF16).ap()
    S["yaT"] = ncd.dram_tensor("s_yaT", [DA, NT0], BF16).ap()
    S["ybT"] = ncd.dram_tensor("s_ybT", [512, NT0], BF16).ap()
    S["mT"] = ncd.dram_tensor("s_mT", [D, NT0], BF16).ap()
    S["hidT"] = ncd.dram_tensor("s_hidT", [DFF, NT0], BF16).ap()
    xs = ncd.dram_tensor("s_xs", [NT0, D], F32).ap()

    kb = KB(nc)
    identf = kb.alloc("identf", [128, 128], F32, persistent=True)
    S["identb"] = kb.alloc("identb", [128, 128], BF16, persistent=True)
    S["tril"] = kb.alloc("tril", [128, 128], F32, persistent=True)
    S["nmw"] = kb.alloc("nmw", [128, NMW_COLS], F32, persistent=True)
    S["zerosb"] = kb.alloc("zerosb", [128, 128], BF16, persistent=True)
    S["onesb"] = kb.alloc("onesb", [128, 128], BF16, persistent=True)
    cbs = [kb.alloc(f"cb{i}", [128, 408], F32, persistent=True) for i in range(len(layers))]
    kb.phase_begin("consts")
    kb.op("sp", lambda e: e.dma_start(out=identf[:], in_=c_ident), w=["identf"], dma="ld_identf")
    kb.op("sp", lambda e: e.dma_start(out=S["tril"][:], in_=c_tril), w=["tril"], dma="ld_tril")
    kb.op("sp", lambda e: e.dma_start(out=S["nmw"][:], in_=c_nm), w=["nm"], dma="ld_nm")
    kb.op("pool", lambda e: e.memset(S["zerosb"][:], 0.0), w=["zerosb"])
    for i in range(len(layers)):
        kb.op("sp", lambda e, i=i: e.dma_start(out=cbs[i][:], in_=c_cb[i]), w=[f"cb{i}"], dma=f"ld_cb{i}")
    kb.op("dve", lambda e: e.tensor_copy(out=S["identb"][:], in_=identf[:]), r=["identf"], w=["identb"])
    kb.op("pool", lambda e: e.memset(S["onesb"][:], 1.0), w=["onesb"])
    kb.phase_end()

    nl = len(layers)
    if nl == 1:
        NT, FS = NT0, NT0 - OWN
        S["cb"] = cbs[0]
        if final:
            emit_layer(kb, Ls[0], S, xw, NT, FS, FS // 128, xs, xs, 0)
            phase_norm(kb, {"g_final": g_final}, S, "normf", xs, FS, NT, "g_final", None, None, dst_final=out, dst_off=FS)
        else:
            emit_layer(kb, Ls[0], S, xw, NT, FS, FS // 128, xs, out, FS)
    else:
        NT, FS = NT0, HALO
        S["cb"] = cbs[0]
        emit_layer(kb, Ls[0], S, xw, NT, FS, (NT0 - OWN) // 128, xs, xs, 0)
        S1 = dict(S)
        for k_ in ("hT", "uT", "qT", "kT", "sgT", "yaT", "ybT", "mT", "hidT"):
            S1[k_] = S[k_][:, HALO:]
        S1["V"] = S["V"][HALO:, :]
        S1["cb"] = cbs[1]
        x1 = xs[HALO:, :]
        NT1 = NT0 - HALO
        emit_layer(kb, Ls[1], S1, x1, NT1, NT1 - OWN, (NT1 - OWN) // 128, x1, x1, 0)
        phase_norm(kb, {"g_final": g_final}, S1, "normf", x1, NT1 - OWN, NT1, "g_final", None, None, dst_final=out, dst_off=NT1 - OWN)
    kb.finish()
    return nc


def _const_tables():
    ident = np.eye(128, dtype=np.float32)
    t = np.arange(128)
    tril = (t[None, :] <= t[:, None]).astype(np.float32)
    k = np.arange(128)[:, None]
    nm = np.zeros((128, NMW_COLS), np.float32)
    BIG = -1.0e7
    for g, d in enumerate((1, 4, 16)):
        ncol = (d + 1) * 128
        dist = np.arange(ncol)[None, :] - k
        valid = (dist >= 0) & (dist <= 128 * d) & (dist % d == 0)
        nm[:, NMW_OFF[g]:NMW_OFF[g] + ncol] = np.where(valid, -dist.astype(np.float32), BIG)
    return ident, tril, nm


def _cb_table(first_of_batch):
    cb = np.zeros((128, 408), np.float32)
    hb = -30000.0 if first_of_batch else 0.0
    for h in range(12):
        for dl in range(17):
            v = -SLOPES[h] * 128.0 * dl
            cb[:, (h * 17 + dl) * 2 + 0] = v
            cb[:, (h * 17 + dl) * 2 + 1] = v + hb
    return cb


def _layer_inputs(inputs, li):
    f = lambda a: np.ascontiguousarray(np.asarray(a, dtype=np.float32))
    d = {}
    for nm_ in ("w_ada", "b_ada", "g_mix", "w_in", "b_in", "g_v", "b_v", "w_s", "b_s", "w_oa", "w_ob", "w_out",
                "g_mlp", "w1", "w2", "b2"):
        d[f"{nm_}_{li}"] = f(inputs[nm_][li])
    d[f"b_inT_{li}"] = f(np.asarray(inputs["b_in"][li]).reshape(INC // 128, 128).T)
    d[f"b1T_{li}"] = f(np.asarray(inputs["b1"][li]).reshape(DFF // 128, 128).T)
    return d


_PROG_CACHE = {}


def _window(xb, t0, n_before):
    w = np.zeros((n_before + OWN, D), np.float32)
    lo = t0 - n_before
    if lo >= 0:
        w[:] = xb[lo:t0 + OWN]
    else:
        w[-lo:] = xb[0:t0 + OWN]
    return w


FUSED = True


def kernel(**inputs):
    x = np.asarray(inputs["x"], dtype=np.float32)
    c = np.asarray(inputs["c"], dtype=np.float32)
    ident, tril, nm = _const_tables()
    B, SEQ, _ = x.shape
    per_b = SEQ // OWN
    common = {"g_final": np.ascontiguousarray(np.asarray(inputs["g_final"], np.float32)),
              "c_ident": ident, "c_tril": tril, "c_nm": nm}

    def run(layers, NT0, final, xcur):
        key = (tuple(layers), NT0, final)
        if key not in _PROG_CACHE:
            _PROG_CACHE[key] = build_program(layers, NT0, final)
        nc = _PROG_CACHE[key]
        shared = dict(common)
        for li in layers:
            shared.update(_layer_inputs(inputs, li))
        in_maps = []
        for core in range(N_CORES):
            b, qd = core // per_b, core % per_b
            m = dict(shared)
            m["xw"] = _window(xcur[b], qd * OWN, NT0 - OWN)
            m["cT"] = np.ascontiguousarray(c[b].reshape(16, 128).T)
            m["c_cb"] = np.stack([_cb_table(qd == 0)] * len(layers))
            in_maps.append(m)
        res = run_bass_kernel_spmd(nc, in_maps, core_ids=list(range(N_CORES)))
        o = np.empty_like(xcur)
        for core in range(N_CORES):
            b, qd = core // per_b, core % per_b
            o[b, qd * OWN:(qd + 1) * OWN] = res.results[core]["out"]
        return o

    if FUSED:
        return run([0, 1], OWN + 2 * HALO, True, x)
    x1 = run([0], OWN + HALO, False, x)
    return run([1], OWN + HALO, True, x1)
```
